# Optimizing a Trainium2 kernel written in Bass

```python
import jax, jax.numpy as jnp
from jax import lax
import numpy as np

D_MODEL = 1024
BATCH = 32
SEQ = 256
DEPTH = 4
DEC_BATCH = 2
DEC_SEQ = 1024
PAST_LEN = 256

GRID_W = 64
HEAD_DIM = 64
A_HEADS = 8
A_KV_HEADS = 2
B_HEADS = 8
B_KV_HEADS = 2
WINDOW = 128
Q_BLOCK = 128
ROPE_BASE = 10000.0
A_WIDTH = A_HEADS * HEAD_DIM
B_WIDTH = B_HEADS * HEAD_DIM
KV_A = A_KV_HEADS * HEAD_DIM
KV_B = B_KV_HEADS * HEAD_DIM
EVEN_IN = 2 * A_WIDTH + 2 * KV_A + 2 * B_WIDTH + 2 * KV_B
EVEN_MIX = A_WIDTH + B_WIDTH
C_WIDTH = D_MODEL
C_GROUPS = 4
C_GROUP_DIM = C_WIDTH // C_GROUPS
ODD_IN = 2 * C_WIDTH
N_EVEN = (DEPTH + 1) // 2
N_ODD = DEPTH // 2
EPS = 1e-6
NEG_BIG = -1e30

kernel_name = "hybrid_dit_prefix_attn_fourier_step"


def rms_norm(x, g):
    xf = x.astype(jnp.float32)
    y = xf * lax.rsqrt(jnp.mean(xf * xf, axis=-1, keepdims=True) + EPS)
    return (y * g.astype(jnp.float32)).astype(x.dtype)


def ada_mod(cvec, w, b):
    m = jax.nn.silu(cvec) @ w + b
    return jnp.split(m, 3, axis=-1)


def modulate(h, shift, scale):
    return h * (1 + scale) + shift


def grid_rope_tables(n_tok):
    rows = n_tok // GRID_W
    row = jnp.repeat(jnp.arange(rows), GRID_W).astype(jnp.float32)
    col = jnp.tile(jnp.arange(GRID_W), rows).astype(jnp.float32)
    half = HEAD_DIM // 2
    inv = 1.0 / (ROPE_BASE ** (jnp.arange(0, half, 2, dtype=jnp.float32) / half))
    ang = jnp.concatenate([row[:, None] * inv, col[:, None] * inv], axis=-1)
    return jnp.cos(ang), jnp.sin(ang)


def apply_rope(x, cos, sin):
    b, s, h, d = x.shape
    xr = x.astype(jnp.float32).reshape(b, s, h, 2, 2, d // 4)
    x1, x2 = xr[..., 0, :], xr[..., 1, :]
    c = cos.reshape(1, s, 1, 2, d // 4)
    sn = sin.reshape(1, s, 1, 2, d // 4)
    y = jnp.stack([x1 * c - x2 * sn, x2 * c + x1 * sn], axis=-2)
    return y.reshape(b, s, h, d).astype(x.dtype)


def softmax_sink(sc, sink):
    sink = sink.astype(jnp.float32)
    m = jnp.maximum(jnp.max(sc, axis=-1, keepdims=True), sink)
    p = jnp.exp(sc - m)
    return p / (jnp.sum(p, axis=-1, keepdims=True) + jnp.exp(sink - m))


def dense_block_attention(q, k, v, sink=None):
    b, s, h, d = q.shape
    kvh = k.shape[2]
    g = h // kvh
    nb = s // Q_BLOCK
    qb = (q * (d ** -0.5)).reshape(b, nb, Q_BLOCK, kvh, g, d).transpose(1, 0, 2, 3, 4, 5)

    def one_block(qblk):
        sc = jnp.einsum("bqkgd,btkd->bkgqt", qblk, k).astype(jnp.float32)
        if sink is None:
            p = jax.nn.softmax(sc, axis=-1)
        else:
            p = softmax_sink(sc, sink.reshape(1, kvh, g, 1, 1))
        return jnp.einsum("bkgqt,btkd->bqkgd", p.astype(v.dtype), v)

    o = lax.map(one_block, qb)
    return o.transpose(1, 0, 2, 3, 4, 5).reshape(b, s, h * d)


def window_attention_with_ctx(q, k, v, k_ctx, v_ctx, sink):
    b, s, h, d = q.shape
    kvh = k.shape[2]
    g = h // kvh
    nb = s // Q_BLOCK
    nband = 2 * (WINDOW // Q_BLOCK) + 1
    L = nband * Q_BLOCK
    pad = ((0, 0), (WINDOW, WINDOW), (0, 0), (0, 0))
    kp = jnp.pad(k, pad).reshape(b, nb + nband - 1, Q_BLOCK, kvh, d)
    vp = jnp.pad(v, pad).reshape(b, nb + nband - 1, Q_BLOCK, kvh, d)
    kband = jnp.concatenate([kp[:, j:j + nb] for j in range(nband)], axis=2)
    vband = jnp.concatenate([vp[:, j:j + nb] for j in range(nband)], axis=2)
    qi = jnp.arange(Q_BLOCK)[:, None]
    kj = jnp.arange(L)[None, :]
    rel = kj - WINDOW - qi
    kpos = jnp.arange(nb)[:, None, None] * Q_BLOCK + kj[None] - WINDOW
    mask = (jnp.abs(rel)[None] <= WINDOW) & (kpos >= 0) & (kpos < s)
    qb = (q * (d ** -0.5)).reshape(b, nb, Q_BLOCK, kvh, g, d)
    s_loc = jnp.einsum("bnqkgd,bnlkd->bnkgql", qb, kband).astype(jnp.float32)
    s_loc = jnp.where(mask[None, :, None, None], s_loc, NEG_BIG)
    s_ctx = jnp.einsum("bnqkgd,btkd->bnkgqt", qb, k_ctx).astype(jnp.float32)
    p = softmax_sink(jnp.concatenate([s_loc, s_ctx], axis=-1), sink.reshape(1, 1, kvh, g, 1, 1))
    p = p.astype(v.dtype)
    o = (jnp.einsum("bnkgql,bnlkd->bnqkgd", p[..., :L], vband)
         + jnp.einsum("bnkgqt,btkd->bnqkgd", p[..., L:], v_ctx))
    return o.reshape(b, s, h * d)


def split_even(p):
    sizes = [A_WIDTH, KV_A, KV_A, A_WIDTH, B_WIDTH, KV_B, KV_B, B_WIDTH]
    idx = [int(i) for i in np.cumsum(sizes)[:-1]]
    return jnp.split(p, idx, axis=-1)


def even_projections(h, w_in, gq, gk):
    b, s, _ = h.shape
    qa, ka, va, ga, qb, kb, vb, gb = split_even(h @ w_in)
    qa = rms_norm(qa.reshape(b, s, A_HEADS, HEAD_DIM), gq)
    ka = rms_norm(ka.reshape(b, s, A_KV_HEADS, HEAD_DIM), gk)
    va = va.reshape(b, s, A_KV_HEADS, HEAD_DIM)
    qb = qb.reshape(b, s, B_HEADS, HEAD_DIM)
    kb = kb.reshape(b, s, B_KV_HEADS, HEAD_DIM)
    vb = vb.reshape(b, s, B_KV_HEADS, HEAD_DIM)
    return qa, ka, va, ga, qb, kb, vb, gb


def even_ctx(h, w_in, w_out, gq, gk, sink):
    qa, ka, va, ga, qb, kb, vb, gb = even_projections(h, w_in, gq, gk)
    oa = dense_block_attention(qa, ka, va)
    ob = dense_block_attention(qb, kb, vb, sink)
    y = jnp.concatenate([oa * jax.nn.silu(ga), ob * jax.nn.silu(gb)], axis=-1) @ w_out
    return y, ka, va, kb, vb


def even_lat(h, w_in, w_out, gq, gk, sink, ka_ctx, va_ctx, kb_ctx, vb_ctx, cos, sin):
    qa, ka, va, ga, qb, kb, vb, gb = even_projections(h, w_in, gq, gk)
    qa, ka = apply_rope(qa, cos, sin), apply_rope(ka, cos, sin)
    qb, kb = apply_rope(qb, cos, sin), apply_rope(kb, cos, sin)
    oa = dense_block_attention(qa, jnp.concatenate([ka, ka_ctx], axis=1),
                               jnp.concatenate([va, va_ctx], axis=1))
    ob = window_attention_with_ctx(qb, kb, vb, kb_ctx, vb_ctx, sink)
    return jnp.concatenate([oa * jax.nn.silu(ga), ob * jax.nn.silu(gb)], axis=-1) @ w_out


def fourier_mix(u):
    b, s, _ = u.shape
    ug = u.astype(jnp.float32).reshape(b, s, C_GROUPS, C_GROUP_DIM)
    f = jnp.fft.fft2(ug, axes=(1, 3), norm="ortho")
    return jnp.real(f).reshape(b, s, C_WIDTH).astype(u.dtype)


def odd_mix(h, w_in, w_out):
    u, gate = jnp.split(h @ w_in, 2, axis=-1)
    return (fourier_mix(u) * jax.nn.silu(gate)) @ w_out


def setup_inputs(seed: int = 0) -> dict:
    key = jax.random.key(seed)
    ks = jax.random.split(key, 20)

    def nrm(k, shape, s):
        return jax.random.normal(k, shape, jnp.float32) * s

    cache_shape = (DEC_BATCH, N_EVEN, PAST_LEN, A_KV_HEADS, HEAD_DIM)
    return {
        "x_prompt": nrm(ks[0], (BATCH, SEQ, D_MODEL), 1.0),
        "x_sample": nrm(ks[1], (DEC_BATCH, DEC_SEQ, D_MODEL), 1.0),
        "cache_k_a": nrm(ks[2], cache_shape, 1.0),
        "cache_v_a": nrm(ks[3], cache_shape, 1.0),
        "cache_k_b": nrm(ks[4], (DEC_BATCH, N_EVEN, PAST_LEN, B_KV_HEADS, HEAD_DIM), 1.0),
        "cache_v_b": nrm(ks[5], (DEC_BATCH, N_EVEN, PAST_LEN, B_KV_HEADS, HEAD_DIM), 1.0),
        "c": nrm(ks[6], (DEC_BATCH, D_MODEL), 1.0),
        "c_ctx": nrm(ks[7], (D_MODEL,), 1.0),
        "norm_g": 1.0 + nrm(ks[8], (DEPTH, D_MODEL), 0.02),
        "ada_w": nrm(ks[9], (DEPTH, D_MODEL, 3 * D_MODEL), 0.5 * D_MODEL ** -0.5),
        "ada_b": nrm(ks[10], (DEPTH, 3 * D_MODEL), 0.02),
        "even_w_in": nrm(ks[11], (N_EVEN, D_MODEL, EVEN_IN), D_MODEL ** -0.5),
        "even_w_out": nrm(ks[12], (N_EVEN, EVEN_MIX, D_MODEL), EVEN_MIX ** -0.5),
        "qk_g_q": 1.0 + nrm(ks[13], (N_EVEN, HEAD_DIM), 0.02),
        "qk_g_k": 1.0 + nrm(ks[14], (N_EVEN, HEAD_DIM), 0.02),
        "sink_logit": nrm(ks[15], (N_EVEN, B_HEADS), 0.5),
        "odd_w_in": nrm(ks[16], (N_ODD, D_MODEL, ODD_IN), D_MODEL ** -0.5),
        "odd_w_out": nrm(ks[17], (N_ODD, C_WIDTH, D_MODEL), C_WIDTH ** -0.5),
        "final_g": 1.0 + nrm(ks[18], (D_MODEL,), 0.02),
    }


def reference(x_prompt, x_sample, cache_k_a, cache_v_a, cache_k_b, cache_v_b, c, c_ctx,
              norm_g, ada_w, ada_b, even_w_in, even_w_out, qk_g_q, qk_g_k, sink_logit,
              odd_w_in, odd_w_out, final_g):
    cos, sin = grid_rope_tables(x_sample.shape[1])
    xc = x_prompt
    xl = x_sample
    ka_list, va_list, kb_list, vb_list = [], [], [], []
    for l in range(DEPTH):
        i = l // 2
        shc, scc, gtc = ada_mod(c_ctx, ada_w[l], ada_b[l])
        shl, scl, gtl = ada_mod(c, ada_w[l], ada_b[l])
        hc = modulate(rms_norm(xc, norm_g[l]), shc, scc)
        hl = modulate(rms_norm(xl, norm_g[l]), shl[:, None], scl[:, None])
        if l % 2 == 0:
            yc, ka, va, kb, vb = even_ctx(hc, even_w_in[i], even_w_out[i],
                                          qk_g_q[i], qk_g_k[i], sink_logit[i])
            ka_list.append(ka)
            va_list.append(va)
            kb_list.append(kb)
            vb_list.append(vb)
            yl = even_lat(hl, even_w_in[i], even_w_out[i], qk_g_q[i], qk_g_k[i], sink_logit[i],
                          cache_k_a[:, i], cache_v_a[:, i], cache_k_b[:, i], cache_v_b[:, i],
                          cos, sin)
        else:
            yc = odd_mix(hc, odd_w_in[i], odd_w_out[i])
            yl = odd_mix(hl, odd_w_in[i], odd_w_out[i])
        xc = xc + gtc * yc
        xl = xl + gtl[:, None] * yl
    y_prompt = rms_norm(xc, final_g)
    y_sample = rms_norm(xl, final_g)
    new_k_a = jnp.stack(ka_list, axis=1)
    new_v_a = jnp.stack(va_list, axis=1)
    new_k_b = jnp.stack(kb_list, axis=1)
    new_v_b = jnp.stack(vb_list, axis=1)
    return (y_prompt, y_sample, new_k_a, new_v_a, new_k_b, new_v_b)
```

```python
import numpy as np
from contextlib import ExitStack
import concourse.bass as bass
import concourse.mybir as mybir
from concourse.bass_utils import run_bass_kernel_spmd

F32 = mybir.dt.float32
BF16 = mybir.dt.bfloat16
ALU = mybir.AluOpType
AF = mybir.ActivationFunctionType
AX = mybir.AxisListType

ENGS = ("pe", "act", "dve", "pool", "sp")
EPS = 1e-6
SEM_LIM = 3000
KSTOP = None
KLAYERS = 4


class _Stop(Exception):
    pass


def stage(n):
    if KSTOP is not None and n > KSTOP:
        raise _Stop()


class Op:
    __slots__ = ("eng", "fn", "deps", "is_dma", "semkey", "semval", "signals", "count", "idx")


def _expand(rs):
    out = []
    for r in rs:
        if isinstance(r, tuple) and len(r) == 2 and r[0] == "ps":
            out.append(("ps", r[1], 0)); out.append(("ps", r[1], 1))
        else:
            out.append(r)
    return out


class Prog:
    def __init__(self, nc):
        self.nc = nc
        self.ops = []
        self.last_w = {}
        self.readers = {}
        self.dma_issued = {}
        self.out_dmas = []

    def _add(self, eng, fn, reads, writes, is_dma=False, semkey=None, extra_deps=()):
        op = Op()
        op.eng = eng; op.fn = fn; op.is_dma = is_dma; op.semkey = semkey
        op.signals = False; op.count = 0; op.semval = 0
        op.idx = len(self.ops)
        reads = _expand(reads); writes = _expand(writes)
        deps = set(extra_deps)
        for r in reads:
            w = self.last_w.get(r)
            if w is not None:
                deps.add(w)
        for r in writes:
            w = self.last_w.get(r)
            if w is not None:
                deps.add(w)
            for rd in self.readers.get(r, ()):
                deps.add(rd)
        deps.discard(op.idx)
        op.deps = deps
        for r in reads:
            lst = self.readers.setdefault(r, [])
            if not is_dma:
                lst[:] = [x for x in lst if self.ops[x].is_dma or self.ops[x].eng != eng]
            lst.append(op.idx)
        for r in writes:
            self.last_w[r] = op.idx
            self.readers[r] = []
        if is_dma:
            n = self.dma_issued.get(semkey, 0) + 1
            self.dma_issued[semkey] = n
            op.semval = 16 * n
        self.ops.append(op)
        return op.idx

    def op(self, eng, fn, reads=(), writes=()):
        return self._add(eng, fn, reads, writes)

    def dma(self, eng, fn, semkey, reads=(), writes=(), is_output=False):
        i = self._add(eng, fn, reads, writes, is_dma=True, semkey=semkey)
        if is_output:
            self.out_dmas.append(i)
        return i

    def finalize(self):
        self._add("sp", None, (), (), extra_deps=tuple(self.out_dmas))
        for op in self.ops:
            for d in op.deps:
                dop = self.ops[d]
                if dop.is_dma:
                    continue
                if dop.eng == op.eng and op.eng == "pe" and not op.is_dma:
                    continue
                dop.signals = True
        cnt = {e: 0 for e in ENGS}
        for op in self.ops:
            if op.signals and not op.is_dma:
                cnt[op.eng] += 1
                op.count = cnt[op.eng]
        self.n_epochs = {e: max(cnt[e] - 1, 0) // SEM_LIM + 1 for e in ENGS}

    def emit_engine(self, eng, e, sems, dma_sems):
        seen = {}
        for op in self.ops:
            if op.eng != eng:
                continue
            waits = {}
            for d in op.deps:
                dop = self.ops[d]
                if dop.is_dma:
                    key = ("d", dop.semkey); val = dop.semval
                else:
                    if dop.eng == eng and eng == "pe" and not op.is_dma:
                        continue
                    key = ("e", dop.eng, (dop.count - 1) // SEM_LIM); val = (dop.count - 1) % SEM_LIM + 1
                if val > waits.get(key, 0):
                    waits[key] = val
            for key, val in waits.items():
                if seen.get(key, 0) >= val:
                    continue
                seen[key] = val
                s = dma_sems[key[1]] if key[0] == "d" else sems[key[1]][key[2]]
                e.wait_ge(s, val)
            if op.fn is None:
                continue
            ins = op.fn(e)
            if op.is_dma:
                ins.then_inc(dma_sems[op.semkey], 16)
            elif op.signals:
                ins.then_inc(sems[eng][(op.count - 1) // SEM_LIM], 1)

    def run_block(self, sem_ctx):
        nc = self.nc
        sems = {e: [sem_ctx("e_%s_%d" % (e, k)) for k in range(self.n_epochs[e])] for e in ENGS}
        dma_sems = {k: sem_ctx("d_" + str(i)) for i, k in enumerate(self.dma_issued)}
        with nc.Block() as block:
            @block.sync
            def _(e):
                self.emit_engine("sp", e, sems, dma_sems)

            @block.tensor
            def _(e):
                self.emit_engine("pe", e, sems, dma_sems)

            @block.scalar
            def _(e):
                self.emit_engine("act", e, sems, dma_sems)

            @block.vector
            def _(e):
                self.emit_engine("dve", e, sems, dma_sems)

            @block.gpsimd
            def _(e):
                self.emit_engine("pool", e, sems, dma_sems)


def chunk_plan():
    ids = {}
    n = 0
    for l in range(4):
        for j in range(6):
            ids[("ada", l, j)] = n; n += 1
    for i in range(2):
        for j in range(5):
            ids[("ein", i, j)] = n; n += 1
        for j in range(2):
            ids[("eout", i, j)] = n; n += 1
    for i in range(2):
        for j in range(4):
            ids[("oin", i, j)] = n; n += 1
        for j in range(2):
            ids[("oout", i, j)] = n; n += 1
    for j in range(4):
        ids[("dft", j)] = n; n += 1
    return ids, n


CH_IDS, NCH = chunk_plan()

CONST_SPECS = [
    ("ident", [128, 128], F32),
    ("cvT", [128, 8, 2], F32),
    ("ng", [128, 4, 8], F32),
    ("fg", [128, 8], F32),
    ("adab", [128, 4, 24], F32),
    ("gq", [128, 2, 64], F32),
    ("gk", [128, 2, 64], F32),
    ("snk", [128, 2, 8], F32),
    ("csc", [128, 2, 512], BF16),
    ("csx", [128, 2, 256], BF16),
    ("nssx", [128, 2, 256], BF16),
    ("ropeC", [128, 8, 64], F32),
    ("ropeSn", [128, 8, 32], F32),
    ("ropeSp", [128, 8, 32], F32),
    ("mlo", [128, 128], BF16),
    ("mhi", [128, 128], BF16),
]


def build_nc():
    nc = bass.Bass("TRN2", target_bir_lowering=False)
    dr = {}
    for name, shape, _ in CONST_SPECS:
        dr[name] = nc.dram_tensor(name, shape, F32, kind="ExternalInput").ap()
    xc_d = nc.dram_tensor("xc", [1024, 1024], F32, kind="ExternalInput").ap()
    xl_d = nc.dram_tensor("xl", [1024, 1024], F32, kind="ExternalInput").ap()
    cache_d = nc.dram_tensor("cache", [2, 4, 256, 128], F32, kind="ExternalInput").ap()
    wst_d = nc.dram_tensor("wst", [NCH, 1024, 512], F32, kind="ExternalInput").ap()
    yc_d = nc.dram_tensor("yc", [1024, 1024], F32, kind="ExternalOutput").ap()
    yl_d = nc.dram_tensor("yl", [1024, 1024], F32, kind="ExternalOutput").ap()
    kvo_d = nc.dram_tensor("kvo", [2, 1024, 512], F32, kind="ExternalOutput").ap()

    P = Prog(nc)
    with ExitStack() as es:
        def sb(name, shape, dt):
            return es.enter_context(nc.sbuf_tensor(name, shape, dt))

        def ps(name, shape, dt):
            return es.enter_context(nc.psum_tensor(name, shape, dt))

        C = {name: sb("c_" + name, shape, dt) for name, shape, dt in CONST_SPECS}
        xT = sb("xT", [128, 8, 1024], F32)
        hT = sb("hT", [128, 8, 1024], BF16)
        bB = sb("bB", [128, 4, 1024], BF16)
        bC = sb("bC", [128, 4, 1024], BF16)
        ring = [sb("ring%d" % k, [128, 8, 512], BF16) for k in range(4)]
        bD = sb("bD", [128, 4, 1024], BF16)
        G = [bB, bD]
        KT = sb("KT", [128, 4, 1280], BF16)
        VA = sb("VA", [128, 10, 2, 384], BF16)
        PTb = [sb("PT%d" % k, [128, 1024], BF16) for k in range(3)]
        O32 = [sb("O32_%d" % k, [128, 512], F32) for k in range(2)]
        S32 = [sb("S32_%d" % k, [128, 512], F32) for k in range(1)] * 2
        R32 = [sb("R32_%d" % k, [128, 512], F32) for k in range(1)] * 2
        PQ = sb("PQ", [128, 8, 2, 512], BF16)
        xst = [sb("xst%d" % k, [128, 1024], F32) for k in range(2)]
        kvst = [sb("kvst%d" % k, [128, 512], F32) for k in range(2)]
        qsq = sb("qsq", [128, 512], F32)
        qf = [sb("qf%d" % k, [128, 512], F32) for k in range(2)]
        qbf = [sb("qbf%d" % k, [128, 512], BF16) for k in range(2)]
        rt1 = sb("rt1", [128, 512], F32)
        rt2 = sb("rt2", [128, 512], F32)
        kdup = [sb("kdup%d" % k, [128, 512], BF16) for k in range(2)]
        ksq = sb("ksq", [128, 128], F32)
        sml = sb("sml", [128, 64], F32)
        rstd = sb("rstd", [128, 1024], F32)
        tmpn = [sb("tmpn%d" % k, [128, 512], F32) for k in range(2)]
        identb = sb("identb", [128, 128], BF16)
        ones = sb("ones", [128, 128], BF16)
        epsc = sb("epsc", [128, 1], F32)
        scT = sb("scT", [128, 8, 2], BF16)
        mod = sb("mod", [128, 4, 24, 2], F32)
        gs = sb("gs", [128, 4, 8, 2], F32)
        gq8 = sb("gq8", [128, 2, 64], F32)
        es_ = sb("es", [128, 2, 8], F32)

        PS = [ps("PS%d" % k, [128, 1024], F32) for k in range(4)]

        def bank_ap(k):
            return PS[k // 2][:, (k % 2) * 512:(k % 2 + 1) * 512]

        st = {"bank": 0, "ring": 0, "pst": 0, "n": 0}
        GEN_BANKS = [0, 1, 2, 3, 4, 5, 6, 7]

        def mm_bank():
            k = GEN_BANKS[st["bank"] % len(GEN_BANKS)]
            st["bank"] += 1
            return k, bank_ap(k)

        def pst_half():
            k, bank = mm_bank()
            return k, bank.bitcast(BF16)[:, 0:512]

        def ring_load(key):
            slot = st["ring"] % 4
            st["ring"] += 1
            cid = CH_IDS[key]
            P.dma("pool", lambda e, slot=slot, cid=cid: e.dma_start(
                out=ring[slot][:], in_=wst_d[cid].rearrange("(kt p) n -> p kt n", p=128)),
                ("ring", slot), writes=[("ring", slot)])
            return slot

        def alt():
            st["n"] += 1
            return st["n"]

        def tcs(tc):
            return slice(tc * 512, (tc + 1) * 512)

        def tts(tt):
            return slice(tt * 128, (tt + 1) * 128)

        for name, shape, dt in CONST_SPECS:
            eng = "pool" if dt == BF16 else "sp"
            P.dma(eng, lambda e, name=name: e.dma_start(out=C[name][:], in_=dr[name]), ("c", name), writes=[("c", name)])
        P.op("dve", lambda e: e.memset(ones[:], 1.0), writes=["ones"])
        P.op("dve", lambda e: e.memset(epsc[:], EPS), writes=["eps"])
        P.op("dve", lambda e: e.memset(VA[:].rearrange("p a b c -> p (a b c)"), 1.0),
             writes=[("va", t, m) for t in range(10) for m in range(2)])
        P.op("dve", lambda e: e.tensor_copy(out=identb[:], in_=C["ident"][:]), reads=[("c", "ident")], writes=["identb"])
        P.op("act", lambda e: e.activation(out=scT[:], in_=C["cvT"][:], func=AF.Silu), reads=[("c", "cvT")], writes=["scT"])
        P.op("act", lambda e: e.activation(out=es_[:], in_=C["snk"][:], func=AF.Exp), reads=[("c", "snk")], writes=["es"])
        P.op("act", lambda e: e.activation(out=gq8[:], in_=C["gq"][:], func=AF.Identity, scale=0.125), reads=[("c", "gq")], writes=["gq8"])

        def ada_chunk(l, j):
            k, bank = mm_bank()
            slot = ring_load(("ada", l, j))
            for t in range(4):
                for kt in range(8):
                    P.op("pe", lambda e, t=t, kt=kt, slot=slot, bank=bank: e.matmul(
                        bank[:, t * 2:t * 2 + 2], lhsT=ring[slot][:, kt, t * 128:(t + 1) * 128],
                        rhs=scT[:, kt, :], start=(kt == 0), stop=(kt == 7)),
                        reads=[("ring", slot), "scT"], writes=[("ps", k)])
            P.op("dve", lambda e, bank=bank: e.tensor_tensor(
                out=mod[:, l, 4 * j:4 * j + 4, :], in0=bank[:, 0:8].rearrange("p (t c) -> p t c", c=2),
                in1=C["adab"][:, l, 4 * j:4 * j + 4].unsqueeze(2).to_broadcast([128, 4, 2]), op=ALU.add),
                reads=[("ps", k), ("c", "adab")], writes=[("mod", l)])
            if j == 5:
                P.op("dve", lambda e: e.scalar_tensor_tensor(
                    out=gs[:, l, :, :], in0=mod[:, l, 8:16, :], scalar=1.0,
                    in1=C["ng"][:, l, :].unsqueeze(2).to_broadcast([128, 8, 2]), op0=ALU.add, op1=ALU.mult),
                    reads=[("mod", l), ("c", "ng")], writes=[("gs", l)])

        ada_q = []

        def ada_step():
            if ada_q:
                ada_chunk(*ada_q.pop(0))

        def load_x(x_d):
            for tt in range(8):
                b = tt % 2
                P.dma("sp", lambda e, tt=tt, b=b: e.dma_start(out=xst[b][:], in_=x_d[tt * 128:(tt + 1) * 128, :]),
                      ("xs", b), writes=[("xst", b)])
                for half in range(2):
                    k, bank = mm_bank()
                    for q in range(4):
                        ft = half * 4 + q
                        P.op("pe", lambda e, q=q, ft=ft, b=b, bank=bank: e.transpose(
                            out=bank[:, q * 128:(q + 1) * 128], in_=xst[b][:, ft * 128:(ft + 1) * 128], identity=C["ident"][:]),
                            reads=[("xst", b), ("c", "ident")], writes=[("ps", k)])
                    wr = [("xT", half * 4 + q, tt // 4) for q in range(4)]
                    if half == 0:
                        P.op("act", lambda e, tt=tt, half=half, bank=bank: e.copy(
                            out=xT[:, half * 4:half * 4 + 4, tts(tt)], in_=bank.rearrange("p (q t) -> p q t", q=4)),
                            reads=[("ps", k)], writes=wr)
                    else:
                        P.op("dve", lambda e, tt=tt, half=half, bank=bank: e.tensor_copy(
                            out=xT[:, half * 4:half * 4 + 4, tts(tt)], in_=bank.rearrange("p (q t) -> p q t", q=4)),
                            reads=[("ps", k)], writes=wr)

        def rms_stats():
            for ft in range(8):
                for tc in range(2):
                    P.op("act", lambda e, ft=ft, tc=tc: e.activation(out=hT[:, ft, tcs(tc)], in_=xT[:, ft, tcs(tc)], func=AF.Square),
                         reads=[("xT", ft, tc)], writes=[("hT", ft, tc)])
            for tc in range(2):
                k, bank = mm_bank()
                for ft in range(8):
                    P.op("pe", lambda e, ft=ft, tc=tc, bank=bank: e.matmul(bank, lhsT=ones[:], rhs=hT[:, ft, tcs(tc)],
                                                                            start=(ft == 0), stop=(ft == 7)),
                         reads=[("hT", ft, tc), "ones"], writes=[("ps", k)])
                P.op("act", lambda e, tc=tc, bank=bank: e.activation(out=tmpn[tc][:], in_=bank, func=AF.Sqrt,
                                                                      bias=epsc[:], scale=1.0 / 1024),
                     reads=[("ps", k), "eps"], writes=[("tmpn", tc)])
                P.op("dve", lambda e, tc=tc: e.reciprocal(out=rstd[:, tcs(tc)], in_=tmpn[tc][:]),
                     reads=[("tmpn", tc)], writes=[("rstd", tc)])

        def norm_mod(l, cond):
            rms_stats()
            for ft in range(8):
                for tc in range(2):
                    b = alt() % 2
                    P.op("dve", lambda e, ft=ft, tc=tc, b=b: e.scalar_tensor_tensor(
                        out=tmpn[b][:], in0=xT[:, ft, tcs(tc)], scalar=gs[:, l, ft, cond:cond + 1],
                        in1=rstd[:, tcs(tc)], op0=ALU.mult, op1=ALU.mult),
                        reads=[("xT", ft, tc), ("gs", l), ("rstd", tc)], writes=[("tmpn", b)])
                    P.op("act", lambda e, ft=ft, tc=tc, b=b: e.activation(
                        out=hT[:, ft, tcs(tc)], in_=tmpn[b][:], func=AF.Identity, bias=mod[:, l, ft, cond:cond + 1], scale=1.0),
                        reads=[("tmpn", b), ("mod", l)], writes=[("hT", ft, tc)])

        def rope_tok(src, H, tt, dst, rd, wr):
            W = H * 64
            t1 = rt1[:, 0:W]
            t2 = rt2[:, 0:W]
            x5 = src.rearrange("p (h j q e) -> p h j q e", h=H, j=2, q=2)
            t5 = t2.rearrange("p (h j q e) -> p h j q e", h=H, j=2, q=2)
            P.op("dve", lambda e: e.tensor_tensor(
                out=t1.rearrange("p (h d) -> p h d", h=H), in0=src.rearrange("p (h d) -> p h d", h=H),
                in1=C["ropeC"][:, tt, :].unsqueeze(1).to_broadcast([128, H, 64]), op=ALU.mult),
                reads=rd + [("c", "ropeC")], writes=["rt1"])
            P.op("dve", lambda e: e.tensor_tensor(
                out=t5[:, :, :, 0, :], in0=x5[:, :, :, 1, :],
                in1=C["ropeSn"][:, tt, :].rearrange("p (j e) -> p j e", j=2).unsqueeze(1).to_broadcast([128, H, 2, 16]), op=ALU.mult),
                reads=rd + [("c", "ropeSn")], writes=["rt2a"])
            P.op("dve", lambda e: e.tensor_tensor(
                out=t5[:, :, :, 1, :], in0=x5[:, :, :, 0, :],
                in1=C["ropeSp"][:, tt, :].rearrange("p (j e) -> p j e", j=2).unsqueeze(1).to_broadcast([128, H, 2, 16]), op=ALU.mult),
                reads=rd + [("c", "ropeSp")], writes=["rt2b"])
            P.op("dve", lambda e: e.tensor_tensor(out=dst, in0=t1, in1=t2, op=ALU.add),
                 reads=["rt1", "rt2a", "rt2b"], writes=wr)

        def k_finish(tile_, ka, kb, va, vb, rd):
            b = alt() % 2
            kd = kdup[b]
            for mi, src in enumerate((ka, kb)):
                P.op("dve" if mi == 0 else "act", (lambda e, mi=mi, src=src: e.tensor_copy(
                    out=kd[:, mi * 256:(mi + 1) * 256].rearrange("p (k r d) -> p k r d", k=2, r=2),
                    in_=src.rearrange("p (k d) -> p k d", k=2).unsqueeze(2).to_broadcast([128, 2, 2, 64]))) if mi == 0 else
                    (lambda e, mi=mi, src=src: e.copy(
                        out=kd[:, mi * 256:(mi + 1) * 256].rearrange("p (k r d) -> p k r d", k=2, r=2),
                        in_=src.rearrange("p (k d) -> p k d", k=2).unsqueeze(2).to_broadcast([128, 2, 2, 64]))),
                    reads=rd, writes=[("kdup", b, mi)])
            def do_tr():
                h, pst = pst_half()
                for n in range(4):
                    P.op("pe", lambda e, n=n, pst=pst: e.transpose(out=pst[:, n * 128:(n + 1) * 128], in_=kd[:, n * 128:(n + 1) * 128],
                                                                   identity=identb[:]),
                         reads=[("kdup", b, n // 2), "identb"], writes=[("ps", h)])
                P.op("act", lambda e, pst=pst: e.copy(out=KT[:, 0:4, tts(tile_)], in_=pst.rearrange("p (n t) -> p n t", n=4)),
                     reads=[("ps", h)], writes=[("kt", n, tile_) for n in range(4)])
            for mi, v in enumerate((va, vb)):
                eng = "dve" if mi == 0 else "act"
                cp = (lambda e, o, i: e.tensor_copy(out=o, in_=i)) if eng == "dve" else (lambda e, o, i: e.copy(out=o, in_=i))
                P.op(eng, lambda e, cp=cp, v=v, mi=mi: cp(e, VA[:, tile_, mi, 0:64], v[:, 0:64]), reads=rd, writes=[("va", tile_, mi)])
                P.op(eng, lambda e, cp=cp, v=v, mi=mi: cp(e, VA[:, tile_, mi, 320:384], v[:, 0:64]), reads=rd, writes=[("va", tile_, mi)])
                P.op(eng, lambda e, cp=cp, v=v, mi=mi: cp(
                    e, VA[:, tile_, mi, 128:256].rearrange("p (r d) -> p r d", r=2),
                    v[:, 64:128].unsqueeze(1).to_broadcast([128, 2, 64])), reads=rd, writes=[("va", tile_, mi)])
            return do_tr

        def attn_phase(i, m, groups):
            steps = []
            for gi, (g, q0, NQ, ktiles, sink) in enumerate(groups):
                for idx, (tile_, mk) in enumerate(ktiles):
                    steps.append((gi, idx, tile_, mk))
            ns = len(steps)

            NQ_ = groups[0][2]
            LAG = 1

            def s_slot(n):
                p = n % 2
                return p, 0, [("ps", 2 * p), ("ps", 2 * p + 1)]

            def pt_slot(n):
                if NQ_ == 128:
                    s6 = n % 6
                    return PTb[s6 // 2][:, (s6 % 2) * 512:(s6 % 2) * 512 + 512], [("pt", s6 // 2, s6 % 2, 0), ("pt", s6 // 2, s6 % 2, 1)]
                return PTb[n % 3][:, :], [("pt", n % 3, 0), ("pt", n % 3, 1)]

            def emit_scores(n):
                gi, idx, tile_, mk = steps[n]
                g, q0, NQ, ktiles, sink = groups[gi]
                p_, coff, sres = s_slot(n)
                Sx = PS[p_]
                hs = [4 * g, 4 * g + 2, 4 * g + 1, 4 * g + 3]
                qtt = list(range(q0 // 128, (q0 + NQ) // 128))
                for slot in (0, 2, 1, 3):
                    h = hs[slot]
                    j = h // 2
                    rows = slice((h % 2) * 64, (h % 2) * 64 + 64)
                    so = (slot // 2) * 512 + coff + (slot % 2) * NQ
                    P.op("pe", lambda e, so=so, j=j, rows=rows, Sx=Sx, tile_=tile_, g=g, q0=q0, NQ=NQ: e.matmul(
                        Sx[:, so:so + NQ], lhsT=KT[rows, m * 2 + g, tts(tile_)], rhs=bC[rows, j, q0:q0 + NQ],
                        start=True, stop=True),
                        reads=[("kt", m * 2 + g, tile_)] + [("bc", j, t) for t in qtt], writes=[sres[slot // 2]])

            def emit_exp(n):
                gi, idx, tile_, mk = steps[n]
                g, q0, NQ, ktiles, sink = groups[gi]
                W = 4 * NQ
                p_, coff, sres = s_slot(n)
                Sx = PS[p_]
                pt, ptres = pt_slot(n)
                for hb in range(2):
                    P.op("act", lambda e, pt=pt, Sx=Sx, NQ=NQ, coff=coff, hb=hb: e.activation(
                        out=pt[:, hb * 2 * NQ:(hb + 1) * 2 * NQ],
                        in_=Sx[:, hb * 512 + coff:hb * 512 + coff + 2 * NQ], func=AF.Exp),
                        reads=[sres[hb]], writes=[ptres[hb]])
                    if mk is not None:
                        P.op("pool", lambda e, pt=pt, mk=mk, NQ=NQ, hb=hb: e.tensor_tensor(
                            out=pt[:, hb * 2 * NQ:(hb + 1) * 2 * NQ].rearrange("p (s q) -> p s q", s=2),
                            in0=pt[:, hb * 2 * NQ:(hb + 1) * 2 * NQ].rearrange("p (s q) -> p s q", s=2),
                            in1=C[mk][:].unsqueeze(1).to_broadcast([128, 2, 128]), op=ALU.mult),
                            reads=[ptres[hb], ("c", mk)], writes=[ptres[hb]])

            def o_aps(gi, NQ):
                Ox = PS[2 + gi % 2]
                return Ox[:, 0:2 * NQ], Ox[:, 512:512 + 2 * NQ], 4 + 2 * (gi % 2), 5 + 2 * (gi % 2)

            def emit_pv(n):
                gi, idx, tile_, mk = steps[n]
                g, q0, NQ, ktiles, sink = groups[gi]
                nk = len(ktiles)
                pt, ptres = pt_slot(n)
                Oe, Oo, ob, ob1 = o_aps(gi, NQ)
                if g == 0:
                    A_ = VA[:, tile_, m, 0:128]; B_ = VA[:, tile_, m, 256:384]
                else:
                    A_ = VA[:, tile_, m, 192:320]; B_ = VA[:, tile_, m, 64:192]
                P.op("pe", lambda e, A_=A_, pt=pt, idx=idx, Oe=Oe, NQ=NQ, nk=nk: e.matmul(
                    Oe, lhsT=A_, rhs=pt[:, 0:2 * NQ], start=(idx == 0), stop=(idx == nk - 1)),
                    reads=[("va", tile_, m), ptres[0]], writes=[("ps", ob)])
                P.op("pe", lambda e, B_=B_, pt=pt, idx=idx, Oo=Oo, NQ=NQ, nk=nk: e.matmul(
                    Oo, lhsT=B_, rhs=pt[:, 2 * NQ:4 * NQ], start=(idx == 0), stop=(idx == nk - 1)),
                    reads=[("va", tile_, m), ptres[1]], writes=[("ps", ob1)])

            def emit_fin(gi):
                g, q0, NQ, ktiles, sink = groups[gi]
                Oe, Oo, ob, ob1 = o_aps(gi, NQ)
                b = gi % 2
                s32 = S32[b]; r32 = R32[b]; o32 = O32[b]
                N2 = 2 * NQ
                P.op("dve", lambda e: e.tensor_copy(out=s32[0:64, 0:N2], in_=Oe[64:128, :]), reads=[("ps", ob)], writes=[("s32", 0, 0)])
                P.op("dve", lambda e: e.tensor_copy(out=s32[64:128, 0:N2], in_=Oo[0:64, :]), reads=[("ps", ob1)], writes=[("s32", 0, 1)])
                if sink:
                    for s2 in range(2):
                        he = 4 * g + 2 * s2
                        ho = he + 1
                        P.op("pool", lambda e, s2=s2, he=he: e.tensor_scalar(
                            out=s32[0:64, s2 * NQ:(s2 + 1) * NQ], in0=s32[0:64, s2 * NQ:(s2 + 1) * NQ],
                            scalar1=es_[0:64, i, he:he + 1], scalar2=None, op0=ALU.add),
                            reads=[("s32", 0, 0), "es"], writes=[("s32", 0, 0)])
                        P.op("pool", lambda e, s2=s2, ho=ho: e.tensor_scalar(
                            out=s32[64:128, s2 * NQ:(s2 + 1) * NQ], in0=s32[64:128, s2 * NQ:(s2 + 1) * NQ],
                            scalar1=es_[64:128, i, ho:ho + 1], scalar2=None, op0=ALU.add),
                            reads=[("s32", 0, 1), "es"], writes=[("s32", 0, 1)])
                P.op("dve", lambda e: e.reciprocal(out=r32[:, 0:N2], in_=s32[:, 0:N2]),
                     reads=[("s32", 0, 0), ("s32", 0, 1)], writes=[("r32", 0)])
                P.op("dve", lambda e: e.tensor_tensor(out=o32[0:64, 0:N2], in0=Oe[0:64, :], in1=r32[0:64, 0:N2], op=ALU.mult),
                     reads=[("ps", ob), ("r32", 0)], writes=[("o32", b, 0)])
                P.op("dve", lambda e: e.tensor_tensor(out=o32[64:128, 0:N2], in0=Oo[64:128, :], in1=r32[64:128, 0:N2], op=ALU.mult),
                     reads=[("ps", ob1), ("r32", 0)], writes=[("o32", b, 1)])
                P.op("pool", lambda e: e.tensor_tensor(
                    out=G[m][:, 2 * g:2 * g + 2, q0:q0 + NQ], in0=o32[:, 0:N2].rearrange("p (s q) -> p s q", s=2),
                    in1=G[m][:, 2 * g:2 * g + 2, q0:q0 + NQ], op=ALU.mult),
                    reads=[("o32", b, 0), ("o32", b, 1)] + [("g", m, 2 * g + s, q0 // 512) for s in range(2)],
                    writes=[("g", m, 2 * g + s, q0 // 512) for s in range(2)])

            for n in range(min(LAG, ns)):
                emit_scores(n)
            for n in range(ns):
                emit_exp(n)
                if n + LAG < ns:
                    emit_scores(n + LAG)
                emit_pv(n)
                gi, idx = steps[n][0], steps[n][1]
                if idx == len(groups[gi][3]) - 1:
                    emit_fin(gi)

        def out_proj(keyname, i, l, cond):
            for c in range(2):
                slot = ring_load((keyname, i, c))
                for j in range(4):
                    ft = c * 4 + j
                    for tc in range(2):
                        k, bank = mm_bank()
                        for kt in range(8):
                            P.op("pe", lambda e, kt=kt, j=j, tc=tc, slot=slot, bank=bank: e.matmul(
                                bank, lhsT=ring[slot][:, kt, j * 128:(j + 1) * 128], rhs=G[kt // 4][:, kt % 4, tcs(tc)],
                                start=(kt == 0), stop=(kt == 7)),
                                reads=[("ring", slot), ("g", kt // 4, kt % 4, tc)], writes=[("ps", k)])
                        P.op("dve", lambda e, ft=ft, tc=tc, bank=bank: e.scalar_tensor_tensor(
                            out=xT[:, ft, tcs(tc)], in0=bank, scalar=mod[:, l, 16 + ft, cond:cond + 1],
                            in1=xT[:, ft, tcs(tc)], op0=ALU.mult, op1=ALU.add),
                            reads=[("ps", k), ("mod", l), ("xT", ft, tc)], writes=[("xT", ft, tc)])

        def fm_proj(slot, dst, resname, func):
            for gfn in fm_groups(slot, dst, resname, func):
                gfn()

        def fm_groups(slot, dst, resname, func):
            return [(lambda j=j, tc=tc: fm_group(slot, dst, resname, func, j, tc)) for j in range(4) for tc in range(2)]

        def fm_group(slot, dst, resname, func, j, tc):
            if True:
                if True:
                    k, bank = mm_bank()
                    for kt in range(8):
                        P.op("pe", lambda e, kt=kt, j=j, tc=tc, bank=bank: e.matmul(
                            bank, lhsT=ring[slot][:, kt, j * 128:(j + 1) * 128], rhs=hT[:, kt, tcs(tc)],
                            start=(kt == 0), stop=(kt == 7)),
                            reads=[("ring", slot), ("hT", kt, tc)], writes=[("ps", k)])
                    wr = [resname + (j, tc)] if resname != "bc" else [("bc", j, t) for t in range(tc * 4, tc * 4 + 4)]
                    if func is None and alt() % 2 == 0:
                        P.op("dve", lambda e, j=j, tc=tc, bank=bank: e.tensor_copy(out=dst[:, j, tcs(tc)], in_=bank),
                             reads=[("ps", k)], writes=wr)
                    else:
                        f = AF.Copy if func is None else func
                        P.op("act", lambda e, j=j, tc=tc, bank=bank, f=f: e.activation(out=dst[:, j, tcs(tc)], in_=bank, func=f),
                             reads=[("ps", k)], writes=wr)

        def even_layer(i, l, cond, lat, after_a=None, sb_=0):
            stage(sb_ + 4)
            slot = ring_load(("ein", i, 0))
            pend = None
            for tt in range(8):
                k, bank = mm_bank()
                for kt in range(8):
                    P.op("pe", lambda e, kt=kt, tt=tt, bank=bank, slot=slot: e.matmul(bank, lhsT=hT[:, kt, tts(tt)], rhs=ring[slot][:, kt, :],
                                                                            start=(kt == 0), stop=(kt == 7)),
                         reads=[("hT", kt, tt // 4), ("ring", slot)], writes=[("ps", k)])
                b = tt % 2
                sv = kvst[b]
                P.op("act", lambda e, sv=sv, bank=bank: e.copy(out=sv[:], in_=bank), reads=[("ps", k)], writes=[("kvst", b)])
                P.op("dve", lambda e, sv=sv: e.tensor_tensor(out=ksq[:], in0=sv[:, 0:128], in1=sv[:, 0:128], op=ALU.mult),
                     reads=[("kvst", b)], writes=["ksq"])
                P.op("dve", lambda e: e.tensor_reduce(out=sml[:, 0:2], in_=ksq[:].rearrange("p (h d) -> p h d", h=2), axis=AX.X, op=ALU.add),
                     reads=["ksq"], writes=["sml_k0"])
                P.op("act", lambda e: e.activation(out=sml[:, 2:4], in_=sml[:, 0:2], func=AF.Sqrt, bias=epsc[:], scale=1.0 / 64),
                     reads=["sml_k0", "eps"], writes=["sml_k1"])
                P.op("dve", lambda e: e.reciprocal(out=sml[:, 4:6], in_=sml[:, 2:4]), reads=["sml_k1"], writes=["sml_k2"])
                P.op("dve", lambda e, sv=sv: e.tensor_tensor(
                    out=sv[:, 0:128].rearrange("p (h d) -> p h d", h=2), in0=sv[:, 0:128].rearrange("p (h d) -> p h d", h=2),
                    in1=sml[:, 4:6].unsqueeze(2).to_broadcast([128, 2, 64]), op=ALU.mult),
                    reads=[("kvst", b), "sml_k2"], writes=[("kvst", b)])
                P.op("dve", lambda e, sv=sv: e.tensor_tensor(
                    out=sv[:, 0:128].rearrange("p (h d) -> p h d", h=2), in0=sv[:, 0:128].rearrange("p (h d) -> p h d", h=2),
                    in1=C["gk"][:, i, :].unsqueeze(1).to_broadcast([128, 2, 64]), op=ALU.mult),
                    reads=[("kvst", b), ("c", "gk")], writes=[("kvst", b)])
                if lat:
                    rope_tok(sv[:, 0:128], 2, tt, sv[:, 0:128], [("kvst", b)], [("kvst", b)])
                    rope_tok(sv[:, 256:384], 2, tt, sv[:, 256:384], [("kvst", b)], [("kvst", b)])
                else:
                    P.dma("sp", lambda e, tt=tt, sv=sv: e.dma_start(out=kvo_d[i, tt * 128:(tt + 1) * 128, :], in_=sv[:]),
                          ("kvo", b), reads=[("kvst", b)], is_output=True)
                dtr = k_finish(tt, sv[:, 0:128], sv[:, 256:384], sv[:, 128:256], sv[:, 384:512], [("kvst", b)])
                if pend is not None:
                    pend()
                pend = dtr
            if pend is not None:
                pend()
            ada_step()
            stage(sb_ + 5)
            if lat:
                for t in range(2):
                    sv = kvst[t]
                    for kind in range(4):
                        P.dma("sp", lambda e, kind=kind, t=t, sv=sv: e.dma_start(
                            out=sv[:, kind * 128:(kind + 1) * 128], in_=cache_d[i, kind, t * 128:(t + 1) * 128, :]),
                            ("cache", t, kind), writes=[("kvst", t)])
                    k_finish(8 + t, sv[:, 0:128], sv[:, 256:384], sv[:, 128:256], sv[:, 384:512], [("kvst", t)])()
            for m in range(2):
                stage(sb_ + 6)
                slot = ring_load(("ein", i, 1 + 2 * m))
                gslot = ring_load(("ein", i, 2 + 2 * m))
                ggroups = fm_groups(gslot, G[m], ("g", m), AF.Silu)
                qpend = None
                for tt in range(8):
                    k, bank = mm_bank()
                    for kt in range(8):
                        P.op("pe", lambda e, kt=kt, tt=tt, bank=bank, slot=slot: e.matmul(
                            bank, lhsT=hT[:, kt, tts(tt)], rhs=ring[slot][:, kt, :], start=(kt == 0), stop=(kt == 7)),
                            reads=[("hT", kt, tt // 4), ("ring", slot)], writes=[("ps", k)])
                    b = tt % 2
                    qb_ = qbf[b]
                    fin = qf[b] if lat else qb_
                    finres = ("qf", b) if lat else ("qbf", b)
                    if m == 0:
                        P.op("act", lambda e, bank=bank: e.activation(out=qsq[:], in_=bank, func=AF.Square), reads=[("ps", k)], writes=["qsq"])
                        P.op("dve", lambda e: e.tensor_reduce(out=sml[:, 8:16], in_=qsq[:].rearrange("p (h d) -> p h d", h=8), axis=AX.X, op=ALU.add),
                             reads=["qsq"], writes=["sml_q0"])
                        P.op("act", lambda e: e.activation(out=sml[:, 16:24], in_=sml[:, 8:16], func=AF.Sqrt, bias=epsc[:], scale=1.0 / 64),
                             reads=["sml_q0", "eps"], writes=["sml_q1"])
                        P.op("dve", lambda e: e.reciprocal(out=sml[:, 24:32], in_=sml[:, 16:24]), reads=["sml_q1"], writes=["sml_q2"])
                        P.op("dve", lambda e, bank=bank: e.tensor_tensor(
                            out=qsq[:].rearrange("p (h d) -> p h d", h=8), in0=bank.rearrange("p (h d) -> p h d", h=8),
                            in1=sml[:, 24:32].unsqueeze(2).to_broadcast([128, 8, 64]), op=ALU.mult),
                            reads=[("ps", k), "sml_q2", "qsq"], writes=["qsq"])
                        P.op("dve", lambda e, fin=fin: e.tensor_tensor(
                            out=fin[:].rearrange("p (h d) -> p h d", h=8), in0=qsq[:].rearrange("p (h d) -> p h d", h=8),
                            in1=gq8[:, i, :].unsqueeze(1).to_broadcast([128, 8, 64]), op=ALU.mult),
                            reads=["qsq", "gq8"], writes=[finres])
                    else:
                        P.op("act", lambda e, bank=bank, fin=fin: e.activation(out=fin[:], in_=bank, func=AF.Identity, scale=0.125),
                             reads=[("ps", k)], writes=[finres])
                    if lat:
                        rope_tok(qf[b][:], 8, tt, qb_[:], [("qf", b)], [("qbf", b)])
                    def q_tr(tt=tt, b=b, qb_=qb_):
                        h, pst = pst_half()
                        for j in range(4):
                            P.op("pe", lambda e, j=j, pst=pst, qb_=qb_: e.transpose(out=pst[:, j * 128:(j + 1) * 128], in_=qb_[:, j * 128:(j + 1) * 128],
                                                                                   identity=identb[:]),
                                 reads=[("qbf", b), "identb"], writes=[("ps", h)])
                        P.op("act", lambda e, pst=pst, tt=tt: e.copy(out=bC[:, 0:4, tts(tt)], in_=pst.rearrange("p (n t) -> p n t", n=4)),
                             reads=[("ps", h)], writes=[("bc", j, tt) for j in range(4)])
                    ggroups[tt]()
                    if qpend is not None:
                        qpend()
                    qpend = q_tr
                if qpend is not None:
                    qpend()
                ada_step()
                stage(sb_ + 7)
                ada_step()
                stage(sb_ + 8)
                groups = []
                if not lat:
                    for s in range(4):
                        for g in range(2):
                            groups.append((g, s * 256, 256, [(2 * s, None), (2 * s + 1, None)], m == 1))
                else:
                    if m == 0:
                        for r2 in range(4):
                            for g in range(2):
                                groups.append((g, r2 * 256, 256, [(t, None) for t in range(10)], False))
                    for r in range(8 if m == 1 else 0):
                        if m == 0:
                            kts = [(t, None) for t in range(10)]
                        else:
                            kts = []
                            if r > 0:
                                kts.append((r - 1, "mlo"))
                            kts.append((r, None))
                            if r < 7:
                                kts.append((r + 1, "mhi"))
                            kts += [(8, None), (9, None)]
                        for g in range(2):
                            groups.append((g, r * 128, 128, kts, m == 1))
                attn_phase(i, m, groups)
                ada_step()
            while ada_q:
                ada_step()
            stage(sb_ + 9)
            out_proj("eout", i, l, cond)

        def odd_layer(i, l, cond, lat, mid=None, sb_=0):
            stage(sb_ + 4)
            for c in range(2):
                slot = ring_load(("oin", i, 2 * c))
                fm_proj(slot, bC, "bc", None)
                ada_step()
                slot = ring_load(("oin", i, 2 * c + 1))
                fm_proj(slot, G[c], ("g", c), AF.Silu)
                ada_step()
                for tt in range(8):
                    for gg in range(2):
                        k, bank = mm_bank()
                        for kk in range(2):
                            P.op("pe", lambda e, kk=kk, gg=gg, tt=tt, bank=bank: e.matmul(
                                bank, lhsT=bC[:, 2 * gg + kk, tts(tt)], rhs=C["csc"][:, kk, :], start=(kk == 0), stop=(kk == 1)),
                                reads=[("bc", 2 * gg + kk, tt), ("c", "csc")], writes=[("ps", k)])
                        if alt() % 2 == 0:
                            P.op("act", lambda e, tt=tt, gg=gg, bank=bank: e.copy(out=PQ[:, tt, gg, :], in_=bank),
                                 reads=[("ps", k)], writes=[("pq", tt, gg)])
                        else:
                            P.op("dve", lambda e, tt=tt, gg=gg, bank=bank: e.tensor_copy(out=PQ[:, tt, gg, :], in_=bank),
                                 reads=[("ps", k)], writes=[("pq", tt, gg)])
                if not lat:
                    for sp_ in range(2):
                        for gg in range(2):
                            for half in range(2):
                                k, bank = mm_bank()
                                for s2 in range(2):
                                    s = 2 * sp_ + s2
                                    for pt in range(2):
                                        P.op("pe", lambda e, s2=s2, s=s, pt=pt, gg=gg, half=half, bank=bank: e.matmul(
                                            bank[:, s2 * 256:(s2 + 1) * 256], lhsT=PQ[:, 2 * s + pt, gg, half * 128:(half + 1) * 128],
                                            rhs=C["csx"][:, pt, :], start=(pt == 0), stop=False),
                                            reads=[("pq", 2 * s + pt, gg), ("c", "csx")], writes=[("ps", k)])
                                        P.op("pe", lambda e, s2=s2, s=s, pt=pt, gg=gg, half=half, bank=bank: e.matmul(
                                            bank[:, s2 * 256:(s2 + 1) * 256], lhsT=PQ[:, 2 * s + pt, gg, 256 + half * 128:256 + (half + 1) * 128],
                                            rhs=C["nssx"][:, pt, :], start=False, stop=(pt == 1)),
                                            reads=[("pq", 2 * s + pt, gg), ("c", "nssx")], writes=[("ps", k)])
                                P.op("dve", lambda e, c=c, gg=gg, half=half, sp_=sp_, bank=bank: e.tensor_tensor(
                                    out=G[c][:, 2 * gg + half, tcs(sp_)], in0=bank, in1=G[c][:, 2 * gg + half, tcs(sp_)], op=ALU.mult),
                                    reads=[("ps", k), ("g", c, 2 * gg + half, sp_)], writes=[("g", c, 2 * gg + half, sp_)])
                else:
                    for tc in range(2):
                        slotC = ring_load(("dft", 2 * tc))
                        slotS = ring_load(("dft", 2 * tc + 1))
                        for gg in range(2):
                            for half in range(2):
                                k, bank = mm_bank()
                                for pt in range(8):
                                    P.op("pe", lambda e, pt=pt, gg=gg, half=half, bank=bank, slotC=slotC: e.matmul(
                                        bank, lhsT=PQ[:, pt, gg, half * 128:(half + 1) * 128], rhs=ring[slotC][:, pt, :],
                                        start=(pt == 0), stop=False),
                                        reads=[("pq", pt, gg), ("ring", slotC)], writes=[("ps", k)])
                                    P.op("pe", lambda e, pt=pt, gg=gg, half=half, bank=bank, slotS=slotS: e.matmul(
                                        bank, lhsT=PQ[:, pt, gg, 256 + half * 128:256 + (half + 1) * 128], rhs=ring[slotS][:, pt, :],
                                        start=False, stop=(pt == 7)),
                                        reads=[("pq", pt, gg), ("ring", slotS)], writes=[("ps", k)])
                                P.op("dve", lambda e, c=c, gg=gg, half=half, tc=tc, bank=bank: e.tensor_tensor(
                                    out=G[c][:, 2 * gg + half, tcs(tc)], in0=bank, in1=G[c][:, 2 * gg + half, tcs(tc)], op=ALU.mult),
                                    reads=[("ps", k), ("g", c, 2 * gg + half, tc)], writes=[("g", c, 2 * gg + half, tc)])
                ada_step()
            while ada_q:
                ada_step()
            out_proj("oout", i, l, cond)

        def final(y_d):
            rms_stats()
            for ft in range(8):
                for tc in range(2):
                    P.op("dve", lambda e, ft=ft, tc=tc: e.scalar_tensor_tensor(
                        out=xT[:, ft, tcs(tc)], in0=xT[:, ft, tcs(tc)], scalar=C["fg"][:, ft:ft + 1],
                        in1=rstd[:, tcs(tc)], op0=ALU.mult, op1=ALU.mult),
                        reads=[("xT", ft, tc), ("c", "fg"), ("rstd", tc)], writes=[("xT", ft, tc)])
            for tt in range(8):
                b = tt % 2
                for half in range(2):
                    k, bank = mm_bank()
                    for q in range(4):
                        ft = half * 4 + q
                        P.op("pe", lambda e, q=q, ft=ft, tt=tt, bank=bank: e.transpose(
                            out=bank[:, q * 128:(q + 1) * 128], in_=xT[:, ft, tts(tt)], identity=C["ident"][:]),
                            reads=[("xT", ft, tt // 4), ("c", "ident")], writes=[("ps", k)])
                    if half == 0:
                        P.op("act", lambda e, b=b, half=half, bank=bank: e.copy(out=xst[b][:, tcs(half)], in_=bank),
                             reads=[("ps", k)], writes=[("xst", b)])
                    else:
                        P.op("dve", lambda e, b=b, half=half, bank=bank: e.tensor_copy(out=xst[b][:, tcs(half)], in_=bank),
                             reads=[("ps", k)], writes=[("xst", b)])
                P.dma("sp", lambda e, tt=tt, b=b: e.dma_start(out=y_d[tt * 128:(tt + 1) * 128, :], in_=xst[b][:]),
                      ("yst", b), reads=[("xst", b)], is_output=True)

        try:
            stage(1)
            for j in range(6):
                ada_chunk(0, j)
            for lat in (False, True):
                cond = 1 if lat else 0
                stage(2 + 100 * lat)
                load_x(xl_d if lat else xc_d)
                for l in range(KLAYERS):
                    stage(3 + 10 * l + 100 * lat)
                    norm_mod(l, cond)
                    if not lat and l < 3:
                        ada_q.extend((l + 1, j) for j in range(6))
                    if l % 2 == 0:
                        even_layer(l // 2, l, cond, lat, sb_=10 * l + 100 * lat)
                    else:
                        odd_layer(l // 2, l, cond, lat, sb_=10 * l + 100 * lat)
                stage(50 + 100 * lat)
                final(yl_d if lat else yc_d)
        except _Stop:
            pass

        P.finalize()
        P.run_block(lambda name: es.enter_context(nc.semaphore(name)))
    return nc


def _fm(v):
    v = np.asarray(v, np.float32)
    lead = v.shape[:-1]
    k = v.shape[-1] // 128
    a = v.reshape(*lead, k, 128)
    a = np.moveaxis(a, -1, 0)
    return np.ascontiguousarray(a)


def _tables():
    t = {}
    t["ident"] = np.eye(128, dtype=np.float32)
    n = np.arange(256)
    ang = 2 * np.pi * ((n[:, None] * n[None, :]) % 256) / 256.0
    cc = np.cos(ang) / 16.0
    sc = np.sin(ang) / 16.0
    csc = np.concatenate([cc, sc], axis=1).astype(np.float32)
    t["csc"] = np.ascontiguousarray(csc.reshape(2, 128, 512).transpose(1, 0, 2))
    t["csx"] = np.ascontiguousarray(cc.astype(np.float32).reshape(2, 128, 256).transpose(1, 0, 2))
    t["nssx"] = np.ascontiguousarray((-sc).astype(np.float32).reshape(2, 128, 256).transpose(1, 0, 2))
    n = np.arange(1024)
    ang = 2 * np.pi * ((n[:, None] * n[None, :]) % 1024) / 1024.0
    t["C1024"] = (np.cos(ang) / 32.0).astype(np.float32)
    t["nS1024"] = (-np.sin(ang) / 32.0).astype(np.float32)
    rows = 1024 // 64
    row = np.repeat(np.arange(rows), 64).astype(np.float32)
    col = np.tile(np.arange(64), rows).astype(np.float32)
    inv = (1.0 / (np.float32(10000.0) ** (np.arange(0, 32, 2, dtype=np.float32) / np.float32(32)))).astype(np.float32)
    angr = (row[:, None] * inv).astype(np.float32)
    angc = (col[:, None] * inv).astype(np.float32)
    cr, sr, cl, sl = np.cos(angr), np.sin(angr), np.cos(angc), np.sin(angc)
    Cf = np.concatenate([cr, cr, cl, cl], axis=1).astype(np.float32)
    Sn = np.concatenate([-sr, -sl], axis=1).astype(np.float32)
    Sp = np.concatenate([sr, sl], axis=1).astype(np.float32)
    t["ropeC"] = np.ascontiguousarray(Cf.reshape(8, 128, 64).transpose(1, 0, 2))
    t["ropeSn"] = np.ascontiguousarray(Sn.reshape(8, 128, 32).transpose(1, 0, 2))
    t["ropeSp"] = np.ascontiguousarray(Sp.reshape(8, 128, 32).transpose(1, 0, 2))
    ii = np.arange(128)
    t["mlo"] = (ii[:, None] >= ii[None, :]).astype(np.float32)
    t["mhi"] = (ii[:, None] <= ii[None, :]).astype(np.float32)
    return t


_NC_CACHE = {}


def kernel(x_prompt, x_sample, cache_k_a, cache_v_a, cache_k_b, cache_v_b, c, c_ctx,
           norm_g, ada_w, ada_b, even_w_in, even_w_out, qk_g_q, qk_g_k, sink_logit,
           odd_w_in, odd_w_out, final_g):
    f = lambda a: np.asarray(a, np.float32)
    x_prompt, x_sample = f(x_prompt), f(x_sample)
    tb = _tables()
    wst = np.empty((NCH, 1024, 512), np.float32)
    ada_w = f(ada_w)
    for l in range(4):
        for j in range(6):
            wst[CH_IDS[("ada", l, j)]] = ada_w[l][:, j * 512:(j + 1) * 512]
    ewi, ewo, owi, owo = f(even_w_in), f(even_w_out), f(odd_w_in), f(odd_w_out)
    for i in range(2):
        w = ewi[i]
        qa, ka, va, ga = w[:, 0:512], w[:, 512:640], w[:, 640:768], w[:, 768:1280]
        qb, kb, vb, gb = w[:, 1280:1792], w[:, 1792:1920], w[:, 1920:2048], w[:, 2048:2560]
        chunks = [np.concatenate([ka, va, kb, vb], axis=1), qa, ga, qb, gb]
        for j in range(5):
            wst[CH_IDS[("ein", i, j)]] = chunks[j]
        for j in range(2):
            wst[CH_IDS[("eout", i, j)]] = ewo[i][:, j * 512:(j + 1) * 512]
        w = owi[i]
        chunks = [w[:, 0:512], w[:, 1024:1536], w[:, 512:1024], w[:, 1536:2048]]
        for j in range(4):
            wst[CH_IDS[("oin", i, j)]] = chunks[j]
        for j in range(2):
            wst[CH_IDS[("oout", i, j)]] = owo[i][:, j * 512:(j + 1) * 512]
    for tc in range(2):
        wst[CH_IDS[("dft", 2 * tc)]] = tb["C1024"][:, tc * 512:(tc + 1) * 512]
        wst[CH_IDS[("dft", 2 * tc + 1)]] = tb["nS1024"][:, tc * 512:(tc + 1) * 512]

    shared = {
        "ident": tb["ident"], "csc": tb["csc"], "csx": tb["csx"], "nssx": tb["nssx"],
        "ropeC": tb["ropeC"], "ropeSn": tb["ropeSn"], "ropeSp": tb["ropeSp"], "mlo": tb["mlo"], "mhi": tb["mhi"],
        "ng": _fm(norm_g), "fg": _fm(final_g), "adab": _fm(ada_b),
        "gq": np.ascontiguousarray(np.broadcast_to(f(qk_g_q)[None], (128, 2, 64))),
        "gk": np.ascontiguousarray(np.broadcast_to(f(qk_g_k)[None], (128, 2, 64))),
        "snk": np.ascontiguousarray(np.broadcast_to(f(sink_logit)[None], (128, 2, 8))),
        "wst": wst,
    }
    caches = [f(cache_k_a), f(cache_v_a), f(cache_k_b), f(cache_v_b)]
    in_maps = []
    for core in range(8):
        b = core % 2
        d = dict(shared)
        d["xc"] = np.ascontiguousarray(x_prompt[4 * core:4 * core + 4].reshape(1024, 1024))
        d["xl"] = np.ascontiguousarray(x_sample[b])
        d["cache"] = np.ascontiguousarray(np.stack([np.stack([cc_[b, i].reshape(256, 128) for cc_ in caches]) for i in range(2)]))
        cv = np.stack([f(c_ctx), f(c)[b]])
        d["cvT"] = np.ascontiguousarray(cv.reshape(2, 8, 128).transpose(2, 1, 0))
        in_maps.append(d)

    if "nc" not in _NC_CACHE:
        _NC_CACHE["nc"] = build_nc()
    nc = _NC_CACHE["nc"]
    res = run_bass_kernel_spmd(nc, in_maps, core_ids=list(range(8)))
    outs = res.results

    y_prompt = np.empty((32, 256, 1024), np.float32)
    y_sample = np.empty((2, 1024, 1024), np.float32)
    nk = [np.empty((32, 2, 256, 2, 64), np.float32) for _ in range(4)]
    for core in range(8):
        r = outs[core]
        y_prompt[4 * core:4 * core + 4] = np.asarray(r["yc"]).reshape(4, 256, 1024)
        if core < 2:
            y_sample[core] = np.asarray(r["yl"])
        kvo = np.asarray(r["kvo"]).reshape(2, 4, 256, 4, 2, 64)
        for kind in range(4):
            nk[kind][4 * core:4 * core + 4] = kvo[:, :, :, kind].transpose(1, 0, 2, 3, 4)
    return (y_prompt, y_sample, nk[0], nk[1], nk[2], nk[3])
```

```python
import numpy as np
from contextlib import ExitStack
import concourse.bass as bass
import concourse.mybir as mybir
from concourse.bass_utils import run_bass_kernel_spmd

F32 = mybir.dt.float32
BF16 = mybir.dt.bfloat16
ALU = mybir.AluOpType
AF = mybir.ActivationFunctionType
AX = mybir.AxisListType

ENGS = ("pe", "act", "dve", "pool", "sp")
EPS = 1e-6
SEM_LIM = 3000
KSTOP = None
KLAYERS = 4


class _Stop(Exception):
    pass


def stage(n):
    if KSTOP is not None and n > KSTOP:
        raise _Stop()


class Op:
    __slots__ = ("eng", "fn", "deps", "is_dma", "semkey", "semval", "signals", "count", "idx")


def _expand(rs):
    out = []
    for r in rs:
        if isinstance(r, tuple) and len(r) == 2 and r[0] == "ps":
            out.append(("ps", r[1], 0)); out.append(("ps", r[1], 1))
        else:
            out.append(r)
    return out


class Prog:
    def __init__(self, nc):
        self.nc = nc
        self.ops = []
        self.last_w = {}
        self.readers = {}
        self.dma_issued = {}
        self.out_dmas = []

    def _add(self, eng, fn, reads, writes, is_dma=False, semkey=None, extra_deps=()):
        op = Op()
        op.eng = eng; op.fn = fn; op.is_dma = is_dma; op.semkey = semkey
        op.signals = False; op.count = 0; op.semval = 0
        op.idx = len(self.ops)
        reads = _expand(reads); writes = _expand(writes)
        deps = set(extra_deps)
        for r in reads:
            w = self.last_w.get(r)
            if w is not None:
                deps.add(w)
        for r in writes:
            w = self.last_w.get(r)
            if w is not None:
                deps.add(w)
            for rd in self.readers.get(r, ()):
                deps.add(rd)
        deps.discard(op.idx)
        op.deps = deps
        for r in reads:
            lst = self.readers.setdefault(r, [])
            if not is_dma:
                lst[:] = [x for x in lst if self.ops[x].is_dma or self.ops[x].eng != eng]
            lst.append(op.idx)
        for r in writes:
            self.last_w[r] = op.idx
            self.readers[r] = []
        if is_dma:
            n = self.dma_issued.get(semkey, 0) + 1
            self.dma_issued[semkey] = n
            op.semval = 16 * n
        self.ops.append(op)
        return op.idx

    def op(self, eng, fn, reads=(), writes=()):
        return self._add(eng, fn, reads, writes)

    def dma(self, eng, fn, semkey, reads=(), writes=(), is_output=False):
        i = self._add(eng, fn, reads, writes, is_dma=True, semkey=semkey)
        if is_output:
            self.out_dmas.append(i)
        return i

    def finalize(self):
        self._add("sp", None, (), (), extra_deps=tuple(self.out_dmas))
        for op in self.ops:
            for d in op.deps:
                dop = self.ops[d]
                if dop.is_dma:
                    continue
                if dop.eng == op.eng and op.eng == "pe" and not op.is_dma:
                    continue
                dop.signals = True
        cnt = {e: 0 for e in ENGS}
        for op in self.ops:
            if op.signals and not op.is_dma:
                cnt[op.eng] += 1
                op.count = cnt[op.eng]
        self.n_epochs = {e: max(cnt[e] - 1, 0) // SEM_LIM + 1 for e in ENGS}

    def emit_engine(self, eng, e, sems, dma_sems):
        seen = {}
        for op in self.ops:
            if op.eng != eng:
                continue
            waits = {}
            for d in op.deps:
                dop = self.ops[d]
                if dop.is_dma:
                    key = ("d", dop.semkey); val = dop.semval
                else:
                    if dop.eng == eng and eng == "pe" and not op.is_dma:
                        continue
                    key = ("e", dop.eng, (dop.count - 1) // SEM_LIM); val = (dop.count - 1) % SEM_LIM + 1
                if val > waits.get(key, 0):
                    waits[key] = val
            for key, val in waits.items():
                if seen.get(key, 0) >= val:
                    continue
                seen[key] = val
                s = dma_sems[key[1]] if key[0] == "d" else sems[key[1]][key[2]]
                e.wait_ge(s, val)
            if op.fn is None:
                continue
            ins = op.fn(e)
            if op.is_dma:
                ins.then_inc(dma_sems[op.semkey], 16)
            elif op.signals:
                ins.then_inc(sems[eng][(op.count - 1) // SEM_LIM], 1)

    def run_block(self, sem_ctx):
        nc = self.nc
        sems = {e: [sem_ctx("e_%s_%d" % (e, k)) for k in range(self.n_epochs[e])] for e in ENGS}
        dma_sems = {k: sem_ctx("d_" + str(i)) for i, k in enumerate(self.dma_issued)}
        with nc.Block() as block:
            @block.sync
            def _(e):
                self.emit_engine("sp", e, sems, dma_sems)

            @block.tensor
            def _(e):
                self.emit_engine("pe", e, sems, dma_sems)

            @block.scalar
            def _(e):
                self.emit_engine("act", e, sems, dma_sems)

            @block.vector
            def _(e):
                self.emit_engine("dve", e, sems, dma_sems)

            @block.gpsimd
            def _(e):
                self.emit_engine("pool", e, sems, dma_sems)


def chunk_plan():
    ids = {}
    n = 0
    for l in range(4):
        for j in range(6):
            ids[("ada", l, j)] = n; n += 1
    for i in range(2):
        for j in range(5):
            ids[("ein", i, j)] = n; n += 1
        for j in range(2):
            ids[("eout", i, j)] = n; n += 1
    for i in range(2):
        for j in range(4):
            ids[("oin", i, j)] = n; n += 1
        for j in range(2):
            ids[("oout", i, j)] = n; n += 1
    for j in range(4):
        ids[("dft", j)] = n; n += 1
    return ids, n


CH_IDS, NCH = chunk_plan()

CONST_SPECS = [
    ("ident", [128, 128], F32),
    ("cvT", [128, 8, 2], F32),
    ("ng", [128, 4, 8], F32),
    ("fg", [128, 8], F32),
    ("adab", [128, 4, 24], F32),
    ("gq", [128, 2, 64], F32),
    ("gk", [128, 2, 64], F32),
    ("snk", [128, 2, 8], F32),
    ("csc", [128, 2, 512], BF16),
    ("csx", [128, 2, 256], BF16),
    ("nssx", [128, 2, 256], BF16),
    ("ropeC", [128, 8, 64], F32),
    ("ropeSn", [128, 8, 32], F32),
    ("ropeSp", [128, 8, 32], F32),
    ("mlo", [128, 128], BF16),
    ("mhi", [128, 128], BF16),
]


def build_nc():
    nc = bass.Bass("TRN2", target_bir_lowering=False)
    dr = {}
    for name, shape, _ in CONST_SPECS:
        dr[name] = nc.dram_tensor(name, shape, F32, kind="ExternalInput").ap()
    xc_d = nc.dram_tensor("xc", [1024, 1024], F32, kind="ExternalInput").ap()
    xl_d = nc.dram_tensor("xl", [1024, 1024], F32, kind="ExternalInput").ap()
    cache_d = nc.dram_tensor("cache", [2, 4, 256, 128], F32, kind="ExternalInput").ap()
    wst_d = nc.dram_tensor("wst", [NCH, 1024, 512], F32, kind="ExternalInput").ap()
    yc_d = nc.dram_tensor("yc", [1024, 1024], F32, kind="ExternalOutput").ap()
    yl_d = nc.dram_tensor("yl", [1024, 1024], F32, kind="ExternalOutput").ap()
    kvo_d = nc.dram_tensor("kvo", [2, 1024, 512], F32, kind="ExternalOutput").ap()

    P = Prog(nc)
    with ExitStack() as es:
        def sb(name, shape, dt):
            return es.enter_context(nc.sbuf_tensor(name, shape, dt))

        def ps(name, shape, dt):
            return es.enter_context(nc.psum_tensor(name, shape, dt))

        C = {name: sb("c_" + name, shape, dt) for name, shape, dt in CONST_SPECS}
        xT = sb("xT", [128, 8, 1024], F32)
        hT = sb("hT", [128, 8, 1024], BF16)
        bB = sb("bB", [128, 4, 1024], BF16)
        bC = sb("bC", [128, 4, 1024], BF16)
        ring = [sb("ring%d" % k, [128, 8, 512], BF16) for k in range(4)]
        bD = sb("bD", [128, 4, 1024], BF16)
        G = [bB, bD]
        KT = sb("KT", [128, 4, 1280], BF16)
        VA = sb("VA", [128, 10, 2, 384], BF16)
        PTb = [sb("PT%d" % k, [128, 1024], BF16) for k in range(3)]
        O32 = [sb("O32_%d" % k, [128, 512], F32) for k in range(2)]
        S32 = [sb("S32_%d" % k, [128, 512], F32) for k in range(1)] * 2
        R32 = [sb("R32_%d" % k, [128, 512], F32) for k in range(1)] * 2
        PQ = sb("PQ", [128, 8, 2, 512], BF16)
        xst = [sb("xst%d" % k, [128, 1024], F32) for k in range(2)]
        kvst = [sb("kvst%d" % k, [128, 512], F32) for k in range(2)]
        qsq = sb("qsq", [128, 512], F32)
        qf = [sb("qf%d" % k, [128, 512], F32) for k in range(2)]
        qbf = [sb("qbf%d" % k, [128, 512], BF16) for k in range(2)]
        rt1 = sb("rt1", [128, 512], F32)
        rt2 = sb("rt2", [128, 512], F32)
        kdup = [sb("kdup%d" % k, [128, 512], BF16) for k in range(2)]
        ksq = sb("ksq", [128, 128], F32)
        sml = sb("sml", [128, 64], F32)
        rstd = sb("rstd", [128, 1024], F32)
        tmpn = [sb("tmpn%d" % k, [128, 512], F32) for k in range(2)]
        identb = sb("identb", [128, 128], BF16)
        ones = sb("ones", [128, 128], BF16)
        epsc = sb("epsc", [128, 1], F32)
        scT = sb("scT", [128, 8, 2], BF16)
        mod = sb("mod", [128, 4, 24, 2], F32)
        gs = sb("gs", [128, 4, 8, 2], F32)
        gq8 = sb("gq8", [128, 2, 64], F32)
        es_ = sb("es", [128, 2, 8], F32)

        PS = [ps("PS%d" % k, [128, 1024], F32) for k in range(4)]

        def bank_ap(k):
            return PS[k // 2][:, (k % 2) * 512:(k % 2 + 1) * 512]

        st = {"bank": 0, "ring": 0, "pst": 0, "n": 0}
        GEN_BANKS = [0, 1, 2, 3, 4, 5, 6, 7]

        def mm_bank():
            k = GEN_BANKS[st["bank"] % len(GEN_BANKS)]
            st["bank"] += 1
            return k, bank_ap(k)

        def pst_half():
            k, bank = mm_bank()
            return k, bank.bitcast(BF16)[:, 0:512]

        def ring_load(key):
            slot = st["ring"] % 4
            st["ring"] += 1
            cid = CH_IDS[key]
            P.dma("pool", lambda e, slot=slot, cid=cid: e.dma_start(
                out=ring[slot][:], in_=wst_d[cid].rearrange("(kt p) n -> p kt n", p=128)),
                ("ring", slot), writes=[("ring", slot)])
            return slot

        def alt():
            st["n"] += 1
            return st["n"]

        def tcs(tc):
            return slice(tc * 512, (tc + 1) * 512)

        def tts(tt):
            return slice(tt * 128, (tt + 1) * 128)

        for name, shape, dt in CONST_SPECS:
            eng = "pool" if dt == BF16 else "sp"
            P.dma(eng, lambda e, name=name: e.dma_start(out=C[name][:], in_=dr[name]), ("c", name), writes=[("c", name)])
        P.op("dve", lambda e: e.memset(ones[:], 1.0), writes=["ones"])
        P.op("dve", lambda e: e.memset(epsc[:], EPS), writes=["eps"])
        P.op("dve", lambda e: e.memset(VA[:].rearrange("p a b c -> p (a b c)"), 1.0),
             writes=[("va", t, m) for t in range(10) for m in range(2)])
        P.op("dve", lambda e: e.tensor_copy(out=identb[:], in_=C["ident"][:]), reads=[("c", "ident")], writes=["identb"])
        P.op("act", lambda e: e.activation(out=scT[:], in_=C["cvT"][:], func=AF.Silu), reads=[("c", "cvT")], writes=["scT"])
        P.op("act", lambda e: e.activation(out=es_[:], in_=C["snk"][:], func=AF.Exp), reads=[("c", "snk")], writes=["es"])
        P.op("act", lambda e: e.activation(out=gq8[:], in_=C["gq"][:], func=AF.Identity, scale=0.125), reads=[("c", "gq")], writes=["gq8"])

        def ada_chunk(l, j):
            k, bank = mm_bank()
            slot = ring_load(("ada", l, j))
            for t in range(4):
                for kt in range(8):
                    P.op("pe", lambda e, t=t, kt=kt, slot=slot, bank=bank: e.matmul(
                        bank[:, t * 2:t * 2 + 2], lhsT=ring[slot][:, kt, t * 128:(t + 1) * 128],
                        rhs=scT[:, kt, :], start=(kt == 0), stop=(kt == 7)),
                        reads=[("ring", slot), "scT"], writes=[("ps", k)])
            P.op("dve", lambda e, bank=bank: e.tensor_tensor(
                out=mod[:, l, 4 * j:4 * j + 4, :], in0=bank[:, 0:8].rearrange("p (t c) -> p t c", c=2),
                in1=C["adab"][:, l, 4 * j:4 * j + 4].unsqueeze(2).to_broadcast([128, 4, 2]), op=ALU.add),
                reads=[("ps", k), ("c", "adab")], writes=[("mod", l)])
            if j == 5:
                P.op("dve", lambda e: e.scalar_tensor_tensor(
                    out=gs[:, l, :, :], in0=mod[:, l, 8:16, :], scalar=1.0,
                    in1=C["ng"][:, l, :].unsqueeze(2).to_broadcast([128, 8, 2]), op0=ALU.add, op1=ALU.mult),
                    reads=[("mod", l), ("c", "ng")], writes=[("gs", l)])

        ada_q = []

        def ada_step():
            if ada_q:
                ada_chunk(*ada_q.pop(0))

        def load_x(x_d):
            for tt in range(8):
                b = tt % 2
                P.dma("sp", lambda e, tt=tt, b=b: e.dma_start(out=xst[b][:], in_=x_d[tt * 128:(tt + 1) * 128, :]),
                      ("xs", b), writes=[("xst", b)])
                for half in range(2):
                    k, bank = mm_bank()
                    for q in range(4):
                        ft = half * 4 + q
                        P.op("pe", lambda e, q=q, ft=ft, b=b, bank=bank: e.transpose(
                            out=bank[:, q * 128:(q + 1) * 128], in_=xst[b][:, ft * 128:(ft + 1) * 128], identity=C["ident"][:]),
                            reads=[("xst", b), ("c", "ident")], writes=[("ps", k)])
                    wr = [("xT", half * 4 + q, tt // 4) for q in range(4)]
                    if half == 0:
                        P.op("act", lambda e, tt=tt, half=half, bank=bank: e.copy(
                            out=xT[:, half * 4:half * 4 + 4, tts(tt)], in_=bank.rearrange("p (q t) -> p q t", q=4)),
                            reads=[("ps", k)], writes=wr)
                    else:
                        P.op("dve", lambda e, tt=tt, half=half, bank=bank: e.tensor_copy(
                            out=xT[:, half * 4:half * 4 + 4, tts(tt)], in_=bank.rearrange("p (q t) -> p q t", q=4)),
                            reads=[("ps", k)], writes=wr)

        def rms_stats():
            for ft in range(8):
                for tc in range(2):
                    P.op("act", lambda e, ft=ft, tc=tc: e.activation(out=hT[:, ft, tcs(tc)], in_=xT[:, ft, tcs(tc)], func=AF.Square),
                         reads=[("xT", ft, tc)], writes=[("hT", ft, tc)])
            for tc in range(2):
                k, bank = mm_bank()
                for ft in range(8):
                    P.op("pe", lambda e, ft=ft, tc=tc, bank=bank: e.matmul(bank, lhsT=ones[:], rhs=hT[:, ft, tcs(tc)],
                                                                            start=(ft == 0), stop=(ft == 7)),
                         reads=[("hT", ft, tc), "ones"], writes=[("ps", k)])
                P.op("act", lambda e, tc=tc, bank=bank: e.activation(out=tmpn[tc][:], in_=bank, func=AF.Sqrt,
                                                                      bias=epsc[:], scale=1.0 / 1024),
                     reads=[("ps", k), "eps"], writes=[("tmpn", tc)])
                P.op("dve", lambda e, tc=tc: e.reciprocal(out=rstd[:, tcs(tc)], in_=tmpn[tc][:]),
                     reads=[("tmpn", tc)], writes=[("rstd", tc)])

        def norm_mod(l, cond):
            rms_stats()
            for ft in range(8):
                for tc in range(2):
                    b = alt() % 2
                    P.op("dve", lambda e, ft=ft, tc=tc, b=b: e.scalar_tensor_tensor(
                        out=tmpn[b][:], in0=xT[:, ft, tcs(tc)], scalar=gs[:, l, ft, cond:cond + 1],
                        in1=rstd[:, tcs(tc)], op0=ALU.mult, op1=ALU.mult),
                        reads=[("xT", ft, tc), ("gs", l), ("rstd", tc)], writes=[("tmpn", b)])
                    P.op("act", lambda e, ft=ft, tc=tc, b=b: e.activation(
                        out=hT[:, ft, tcs(tc)], in_=tmpn[b][:], func=AF.Identity, bias=mod[:, l, ft, cond:cond + 1], scale=1.0),
                        reads=[("tmpn", b), ("mod", l)], writes=[("hT", ft, tc)])

        def rope_tok(src, H, tt, dst, rd, wr):
            W = H * 64
            t1 = rt1[:, 0:W]
            t2 = rt2[:, 0:W]
            x5 = src.rearrange("p (h j q e) -> p h j q e", h=H, j=2, q=2)
            t5 = t2.rearrange("p (h j q e) -> p h j q e", h=H, j=2, q=2)
            P.op("dve", lambda e: e.tensor_tensor(
                out=t1.rearrange("p (h d) -> p h d", h=H), in0=src.rearrange("p (h d) -> p h d", h=H),
                in1=C["ropeC"][:, tt, :].unsqueeze(1).to_broadcast([128, H, 64]), op=ALU.mult),
                reads=rd + [("c", "ropeC")], writes=["rt1"])
            P.op("dve", lambda e: e.tensor_tensor(
                out=t5[:, :, :, 0, :], in0=x5[:, :, :, 1, :],
                in1=C["ropeSn"][:, tt, :].rearrange("p (j e) -> p j e", j=2).unsqueeze(1).to_broadcast([128, H, 2, 16]), op=ALU.mult),
                reads=rd + [("c", "ropeSn")], writes=["rt2a"])
            P.op("dve", lambda e: e.tensor_tensor(
                out=t5[:, :, :, 1, :], in0=x5[:, :, :, 0, :],
                in1=C["ropeSp"][:, tt, :].rearrange("p (j e) -> p j e", j=2).unsqueeze(1).to_broadcast([128, H, 2, 16]), op=ALU.mult),
                reads=rd + [("c", "ropeSp")], writes=["rt2b"])
            P.op("dve", lambda e: e.tensor_tensor(out=dst, in0=t1, in1=t2, op=ALU.add),
                 reads=["rt1", "rt2a", "rt2b"], writes=wr)

        def k_finish(tile_, ka, kb, va, vb, rd):
            b = alt() % 2
            kd = kdup[b]
            for mi, src in enumerate((ka, kb)):
                P.op("dve" if mi == 0 else "act", (lambda e, mi=mi, src=src: e.tensor_copy(
                    out=kd[:, mi * 256:(mi + 1) * 256].rearrange("p (k r d) -> p k r d", k=2, r=2),
                    in_=src.rearrange("p (k d) -> p k d", k=2).unsqueeze(2).to_broadcast([128, 2, 2, 64]))) if mi == 0 else
                    (lambda e, mi=mi, src=src: e.copy(
                        out=kd[:, mi * 256:(mi + 1) * 256].rearrange("p (k r d) -> p k r d", k=2, r=2),
                        in_=src.rearrange("p (k d) -> p k d", k=2).unsqueeze(2).to_broadcast([128, 2, 2, 64]))),
                    reads=rd, writes=[("kdup", b, mi)])
            def do_tr():
                h, pst = pst_half()
                for n in range(4):
                    P.op("pe", lambda e, n=n, pst=pst: e.transpose(out=pst[:, n * 128:(n + 1) * 128], in_=kd[:, n * 128:(n + 1) * 128],
                                                                   identity=identb[:]),
                         reads=[("kdup", b, n // 2), "identb"], writes=[("ps", h)])
                P.op("act", lambda e, pst=pst: e.copy(out=KT[:, 0:4, tts(tile_)], in_=pst.rearrange("p (n t) -> p n t", n=4)),
                     reads=[("ps", h)], writes=[("kt", n, tile_) for n in range(4)])
            for mi, v in enumerate((va, vb)):
                eng = "dve" if mi == 0 else "act"
                cp = (lambda e, o, i: e.tensor_copy(out=o, in_=i)) if eng == "dve" else (lambda e, o, i: e.copy(out=o, in_=i))
                P.op(eng, lambda e, cp=cp, v=v, mi=mi: cp(e, VA[:, tile_, mi, 0:64], v[:, 0:64]), reads=rd, writes=[("va", tile_, mi)])
                P.op(eng, lambda e, cp=cp, v=v, mi=mi: cp(e, VA[:, tile_, mi, 320:384], v[:, 0:64]), reads=rd, writes=[("va", tile_, mi)])
                P.op(eng, lambda e, cp=cp, v=v, mi=mi: cp(
                    e, VA[:, tile_, mi, 128:256].rearrange("p (r d) -> p r d", r=2),
                    v[:, 64:128].unsqueeze(1).to_broadcast([128, 2, 64])), reads=rd, writes=[("va", tile_, mi)])
            return do_tr

        def attn_phase(i, m, groups):
            steps = []
            for gi, (g, q0, NQ, ktiles, sink) in enumerate(groups):
                for idx, (tile_, mk) in enumerate(ktiles):
                    steps.append((gi, idx, tile_, mk))
            ns = len(steps)

            NQ_ = groups[0][2]
            LAG = 1

            def s_slot(n):
                p = n % 2
                return p, 0, [("ps", 2 * p), ("ps", 2 * p + 1)]

            def pt_slot(n):
                if NQ_ == 128:
                    s6 = n % 6
                    return PTb[s6 // 2][:, (s6 % 2) * 512:(s6 % 2) * 512 + 512], [("pt", s6 // 2, s6 % 2, 0), ("pt", s6 // 2, s6 % 2, 1)]
                return PTb[n % 3][:, :], [("pt", n % 3, 0), ("pt", n % 3, 1)]

            def emit_scores(n):
                gi, idx, tile_, mk = steps[n]
                g, q0, NQ, ktiles, sink = groups[gi]
                p_, coff, sres = s_slot(n)
                Sx = PS[p_]
                hs = [4 * g, 4 * g + 2, 4 * g + 1, 4 * g + 3]
                qtt = list(range(q0 // 128, (q0 + NQ) // 128))
                for slot in (0, 2, 1, 3):
                    h = hs[slot]
                    j = h // 2
                    rows = slice((h % 2) * 64, (h % 2) * 64 + 64)
                    so = (slot // 2) * 512 + coff + (slot % 2) * NQ
                    P.op("pe", lambda e, so=so, j=j, rows=rows, Sx=Sx, tile_=tile_, g=g, q0=q0, NQ=NQ: e.matmul(
                        Sx[:, so:so + NQ], lhsT=KT[rows, m * 2 + g, tts(tile_)], rhs=bC[rows, j, q0:q0 + NQ],
                        start=True, stop=True),
                        reads=[("kt", m * 2 + g, tile_)] + [("bc", j, t) for t in qtt], writes=[sres[slot // 2]])

            def emit_exp(n):
                gi, idx, tile_, mk = steps[n]
                g, q0, NQ, ktiles, sink = groups[gi]
                W = 4 * NQ
                p_, coff, sres = s_slot(n)
                Sx = PS[p_]
                pt, ptres = pt_slot(n)
                for hb in range(2):
                    P.op("act", lambda e, pt=pt, Sx=Sx, NQ=NQ, coff=coff, hb=hb: e.activation(
                        out=pt[:, hb * 2 * NQ:(hb + 1) * 2 * NQ],
                        in_=Sx[:, hb * 512 + coff:hb * 512 + coff + 2 * NQ], func=AF.Exp),
                        reads=[sres[hb]], writes=[ptres[hb]])
                    if mk is not None:
                        P.op("pool", lambda e, pt=pt, mk=mk, NQ=NQ, hb=hb: e.tensor_tensor(
                            out=pt[:, hb * 2 * NQ:(hb + 1) * 2 * NQ].rearrange("p (s q) -> p s q", s=2),
                            in0=pt[:, hb * 2 * NQ:(hb + 1) * 2 * NQ].rearrange("p (s q) -> p s q", s=2),
                            in1=C[mk][:].unsqueeze(1).to_broadcast([128, 2, 128]), op=ALU.mult),
                            reads=[ptres[hb], ("c", mk)], writes=[ptres[hb]])

            def o_aps(gi, NQ):
                Ox = PS[2 + gi % 2]
                return Ox[:, 0:2 * NQ], Ox[:, 512:512 + 2 * NQ], 4 + 2 * (gi % 2), 5 + 2 * (gi % 2)

            def emit_pv(n):
                gi, idx, tile_, mk = steps[n]
                g, q0, NQ, ktiles, sink = groups[gi]
                nk = len(ktiles)
                pt, ptres = pt_slot(n)
                Oe, Oo, ob, ob1 = o_aps(gi, NQ)
                if g == 0:
                    A_ = VA[:, tile_, m, 0:128]; B_ = VA[:, tile_, m, 256:384]
                else:
                    A_ = VA[:, tile_, m, 192:320]; B_ = VA[:, tile_, m, 64:192]
                P.op("pe", lambda e, A_=A_, pt=pt, idx=idx, Oe=Oe, NQ=NQ, nk=nk: e.matmul(
                    Oe, lhsT=A_, rhs=pt[:, 0:2 * NQ], start=(idx == 0), stop=(idx == nk - 1)),
                    reads=[("va", tile_, m), ptres[0]], writes=[("ps", ob)])
                P.op("pe", lambda e, B_=B_, pt=pt, idx=idx, Oo=Oo, NQ=NQ, nk=nk: e.matmul(
                    Oo, lhsT=B_, rhs=pt[:, 2 * NQ:4 * NQ], start=(idx == 0), stop=(idx == nk - 1)),
                    reads=[("va", tile_, m), ptres[1]], writes=[("ps", ob1)])

            def emit_fin(gi):
                g, q0, NQ, ktiles, sink = groups[gi]
                Oe, Oo, ob, ob1 = o_aps(gi, NQ)
                b = gi % 2
                s32 = S32[b]; r32 = R32[b]; o32 = O32[b]
                N2 = 2 * NQ
                P.op("dve", lambda e: e.tensor_copy(out=s32[0:64, 0:N2], in_=Oe[64:128, :]), reads=[("ps", ob)], writes=[("s32", 0, 0)])
                P.op("dve", lambda e: e.tensor_copy(out=s32[64:128, 0:N2], in_=Oo[0:64, :]), reads=[("ps", ob1)], writes=[("s32", 0, 1)])
                if sink:
                    for s2 in range(2):
                        he = 4 * g + 2 * s2
                        ho = he + 1
                        P.op("dve", lambda e, s2=s2, he=he: e.tensor_scalar(
                            out=s32[0:64, s2 * NQ:(s2 + 1) * NQ], in0=s32[0:64, s2 * NQ:(s2 + 1) * NQ],
                            scalar1=es_[0:64, i, he:he + 1], scalar2=None, op0=ALU.add),
                            reads=[("s32", 0, 0), "es"], writes=[("s32", 0, 0)])
                        P.op("dve", lambda e, s2=s2, ho=ho: e.tensor_scalar(
                            out=s32[64:128, s2 * NQ:(s2 + 1) * NQ], in0=s32[64:128, s2 * NQ:(s2 + 1) * NQ],
                            scalar1=es_[64:128, i, ho:ho + 1], scalar2=None, op0=ALU.add),
                            reads=[("s32", 0, 1), "es"], writes=[("s32", 0, 1)])
                P.op("dve", lambda e: e.reciprocal(out=r32[:, 0:N2], in_=s32[:, 0:N2]),
                     reads=[("s32", 0, 0), ("s32", 0, 1)], writes=[("r32", 0)])
                P.op("dve", lambda e: e.tensor_tensor(out=o32[0:64, 0:N2], in0=Oe[0:64, :], in1=r32[0:64, 0:N2], op=ALU.mult),
                     reads=[("ps", ob), ("r32", 0)], writes=[("o32", b, 0)])
                P.op("dve", lambda e: e.tensor_tensor(out=o32[64:128, 0:N2], in0=Oo[64:128, :], in1=r32[64:128, 0:N2], op=ALU.mult),
                     reads=[("ps", ob1), ("r32", 0)], writes=[("o32", b, 1)])
                P.op("dve", lambda e: e.tensor_tensor(
                    out=G[m][:, 2 * g:2 * g + 2, q0:q0 + NQ], in0=o32[:, 0:N2].rearrange("p (s q) -> p s q", s=2),
                    in1=G[m][:, 2 * g:2 * g + 2, q0:q0 + NQ], op=ALU.mult),
                    reads=[("o32", b, 0), ("o32", b, 1)] + [("g", m, 2 * g + s, q0 // 512) for s in range(2)],
                    writes=[("g", m, 2 * g + s, q0 // 512) for s in range(2)])

            for n in range(min(LAG, ns)):
                emit_scores(n)
            for n in range(ns):
                emit_exp(n)
                if n + LAG < ns:
                    emit_scores(n + LAG)
                emit_pv(n)
                gi, idx = steps[n][0], steps[n][1]
                if idx == len(groups[gi][3]) - 1:
                    emit_fin(gi)

        def out_proj(keyname, i, l, cond, pre=None):
            for c in range(2):
                slot = pre[c] if pre is not None else ring_load((keyname, i, c))
                for j in range(4):
                    ft = c * 4 + j
                    for tc in range(2):
                        k, bank = mm_bank()
                        for kt in range(8):
                            P.op("pe", lambda e, kt=kt, j=j, tc=tc, slot=slot, bank=bank: e.matmul(
                                bank, lhsT=ring[slot][:, kt, j * 128:(j + 1) * 128], rhs=G[kt // 4][:, kt % 4, tcs(tc)],
                                start=(kt == 0), stop=(kt == 7)),
                                reads=[("ring", slot), ("g", kt // 4, kt % 4, tc)], writes=[("ps", k)])
                        P.op("dve", lambda e, ft=ft, tc=tc, bank=bank: e.scalar_tensor_tensor(
                            out=xT[:, ft, tcs(tc)], in0=bank, scalar=mod[:, l, 16 + ft, cond:cond + 1],
                            in1=xT[:, ft, tcs(tc)], op0=ALU.mult, op1=ALU.add),
                            reads=[("ps", k), ("mod", l), ("xT", ft, tc)], writes=[("xT", ft, tc)])

        def fm_proj(slot, dst, resname, func):
            for gfn in fm_groups(slot, dst, resname, func):
                gfn()

        def fm_groups(slot, dst, resname, func):
            return [(lambda j=j, tc=tc: fm_group(slot, dst, resname, func, j, tc)) for j in range(4) for tc in range(2)]

        def fm_group(slot, dst, resname, func, j, tc):
            if True:
                if True:
                    k, bank = mm_bank()
                    for kt in range(8):
                        P.op("pe", lambda e, kt=kt, j=j, tc=tc, bank=bank: e.matmul(
                            bank, lhsT=ring[slot][:, kt, j * 128:(j + 1) * 128], rhs=hT[:, kt, tcs(tc)],
                            start=(kt == 0), stop=(kt == 7)),
                            reads=[("ring", slot), ("hT", kt, tc)], writes=[("ps", k)])
                    wr = [resname + (j, tc)] if resname != "bc" else [("bc", j, t) for t in range(tc * 4, tc * 4 + 4)]
                    if func is None and alt() % 2 == 0:
                        P.op("dve", lambda e, j=j, tc=tc, bank=bank: e.tensor_copy(out=dst[:, j, tcs(tc)], in_=bank),
                             reads=[("ps", k)], writes=wr)
                    else:
                        f = AF.Copy if func is None else func
                        P.op("act", lambda e, j=j, tc=tc, bank=bank, f=f: e.activation(out=dst[:, j, tcs(tc)], in_=bank, func=f),
                             reads=[("ps", k)], writes=wr)

        def even_layer(i, l, cond, lat, after_a=None, sb_=0):
            stage(sb_ + 4)
            slot = ring_load(("ein", i, 0))
            pend = None
            for tt in range(8):
                k, bank = mm_bank()
                for kt in range(8):
                    P.op("pe", lambda e, kt=kt, tt=tt, bank=bank, slot=slot: e.matmul(bank, lhsT=hT[:, kt, tts(tt)], rhs=ring[slot][:, kt, :],
                                                                            start=(kt == 0), stop=(kt == 7)),
                         reads=[("hT", kt, tt // 4), ("ring", slot)], writes=[("ps", k)])
                b = tt % 2
                sv = kvst[b]
                P.op("act", lambda e, sv=sv, bank=bank: e.copy(out=sv[:], in_=bank), reads=[("ps", k)], writes=[("kvst", b)])
                P.op("dve", lambda e, sv=sv: e.tensor_tensor(out=ksq[:], in0=sv[:, 0:128], in1=sv[:, 0:128], op=ALU.mult),
                     reads=[("kvst", b)], writes=["ksq"])
                P.op("dve", lambda e: e.tensor_reduce(out=sml[:, 0:2], in_=ksq[:].rearrange("p (h d) -> p h d", h=2), axis=AX.X, op=ALU.add),
                     reads=["ksq"], writes=["sml_k0"])
                P.op("act", lambda e: e.activation(out=sml[:, 2:4], in_=sml[:, 0:2], func=AF.Sqrt, bias=epsc[:], scale=1.0 / 64),
                     reads=["sml_k0", "eps"], writes=["sml_k1"])
                P.op("dve", lambda e: e.reciprocal(out=sml[:, 4:6], in_=sml[:, 2:4]), reads=["sml_k1"], writes=["sml_k2"])
                P.op("dve", lambda e, sv=sv: e.tensor_tensor(
                    out=sv[:, 0:128].rearrange("p (h d) -> p h d", h=2), in0=sv[:, 0:128].rearrange("p (h d) -> p h d", h=2),
                    in1=sml[:, 4:6].unsqueeze(2).to_broadcast([128, 2, 64]), op=ALU.mult),
                    reads=[("kvst", b), "sml_k2"], writes=[("kvst", b)])
                P.op("dve", lambda e, sv=sv: e.tensor_tensor(
                    out=sv[:, 0:128].rearrange("p (h d) -> p h d", h=2), in0=sv[:, 0:128].rearrange("p (h d) -> p h d", h=2),
                    in1=C["gk"][:, i, :].unsqueeze(1).to_broadcast([128, 2, 64]), op=ALU.mult),
                    reads=[("kvst", b), ("c", "gk")], writes=[("kvst", b)])
                if lat:
                    rope_tok(sv[:, 0:128], 2, tt, sv[:, 0:128], [("kvst", b)], [("kvst", b)])
                    rope_tok(sv[:, 256:384], 2, tt, sv[:, 256:384], [("kvst", b)], [("kvst", b)])
                else:
                    P.dma("sp", lambda e, tt=tt, sv=sv: e.dma_start(out=kvo_d[i, tt * 128:(tt + 1) * 128, :], in_=sv[:]),
                          ("kvo", b), reads=[("kvst", b)], is_output=True)
                dtr = k_finish(tt, sv[:, 0:128], sv[:, 256:384], sv[:, 128:256], sv[:, 384:512], [("kvst", b)])
                if pend is not None:
                    pend()
                pend = dtr
            if pend is not None:
                pend()
            ada_step()
            stage(sb_ + 5)
            if lat:
                for t in range(2):
                    sv = kvst[t]
                    for kind in range(4):
                        P.dma("sp", lambda e, kind=kind, t=t, sv=sv: e.dma_start(
                            out=sv[:, kind * 128:(kind + 1) * 128], in_=cache_d[i, kind, t * 128:(t + 1) * 128, :]),
                            ("cache", t, kind), writes=[("kvst", t)])
                    k_finish(8 + t, sv[:, 0:128], sv[:, 256:384], sv[:, 128:256], sv[:, 384:512], [("kvst", t)])()
            for m in range(2):
                stage(sb_ + 6)
                slot = ring_load(("ein", i, 1 + 2 * m))
                gslot = ring_load(("ein", i, 2 + 2 * m))
                ggroups = fm_groups(gslot, G[m], ("g", m), AF.Silu)
                qpend = None
                for tt in range(8):
                    k, bank = mm_bank()
                    for kt in range(8):
                        P.op("pe", lambda e, kt=kt, tt=tt, bank=bank, slot=slot: e.matmul(
                            bank, lhsT=hT[:, kt, tts(tt)], rhs=ring[slot][:, kt, :], start=(kt == 0), stop=(kt == 7)),
                            reads=[("hT", kt, tt // 4), ("ring", slot)], writes=[("ps", k)])
                    b = tt % 2
                    qb_ = qbf[b]
                    fin = qf[b] if lat else qb_
                    finres = ("qf", b) if lat else ("qbf", b)
                    if m == 0:
                        P.op("act", lambda e, bank=bank: e.activation(out=qsq[:], in_=bank, func=AF.Square), reads=[("ps", k)], writes=["qsq"])
                        P.op("dve", lambda e: e.tensor_reduce(out=sml[:, 8:16], in_=qsq[:].rearrange("p (h d) -> p h d", h=8), axis=AX.X, op=ALU.add),
                             reads=["qsq"], writes=["sml_q0"])
                        P.op("act", lambda e: e.activation(out=sml[:, 16:24], in_=sml[:, 8:16], func=AF.Sqrt, bias=epsc[:], scale=1.0 / 64),
                             reads=["sml_q0", "eps"], writes=["sml_q1"])
                        P.op("dve", lambda e: e.reciprocal(out=sml[:, 24:32], in_=sml[:, 16:24]), reads=["sml_q1"], writes=["sml_q2"])
                        P.op("dve", lambda e, bank=bank: e.tensor_tensor(
                            out=qsq[:].rearrange("p (h d) -> p h d", h=8), in0=bank.rearrange("p (h d) -> p h d", h=8),
                            in1=sml[:, 24:32].unsqueeze(2).to_broadcast([128, 8, 64]), op=ALU.mult),
                            reads=[("ps", k), "sml_q2", "qsq"], writes=["qsq"])
                        P.op("dve", lambda e, fin=fin: e.tensor_tensor(
                            out=fin[:].rearrange("p (h d) -> p h d", h=8), in0=qsq[:].rearrange("p (h d) -> p h d", h=8),
                            in1=gq8[:, i, :].unsqueeze(1).to_broadcast([128, 8, 64]), op=ALU.mult),
                            reads=["qsq", "gq8"], writes=[finres])
                    else:
                        P.op("act", lambda e, bank=bank, fin=fin: e.activation(out=fin[:], in_=bank, func=AF.Identity, scale=0.125),
                             reads=[("ps", k)], writes=[finres])
                    if lat:
                        rope_tok(qf[b][:], 8, tt, qb_[:], [("qf", b)], [("qbf", b)])
                    def q_tr(tt=tt, b=b, qb_=qb_):
                        h, pst = pst_half()
                        for j in range(4):
                            P.op("pe", lambda e, j=j, pst=pst, qb_=qb_: e.transpose(out=pst[:, j * 128:(j + 1) * 128], in_=qb_[:, j * 128:(j + 1) * 128],
                                                                                   identity=identb[:]),
                                 reads=[("qbf", b), "identb"], writes=[("ps", h)])
                        P.op("act", lambda e, pst=pst, tt=tt: e.copy(out=bC[:, 0:4, tts(tt)], in_=pst.rearrange("p (n t) -> p n t", n=4)),
                             reads=[("ps", h)], writes=[("bc", j, tt) for j in range(4)])
                    ggroups[tt]()
                    if qpend is not None:
                        qpend()
                    qpend = q_tr
                if qpend is not None:
                    qpend()
                ada_step()
                stage(sb_ + 7)
                ada_step()
                stage(sb_ + 8)
                groups = []
                if not lat:
                    for s in range(4):
                        for g in range(2):
                            groups.append((g, s * 256, 256, [(2 * s, None), (2 * s + 1, None)], m == 1))
                else:
                    if m == 0:
                        for r2 in range(4):
                            for g in range(2):
                                groups.append((g, r2 * 256, 256, [(t, None) for t in range(10)], False))
                    for r in range(8 if m == 1 else 0):
                        if m == 0:
                            kts = [(t, None) for t in range(10)]
                        else:
                            kts = []
                            if r > 0:
                                kts.append((r - 1, "mlo"))
                            kts.append((r, None))
                            if r < 7:
                                kts.append((r + 1, "mhi"))
                            kts += [(8, None), (9, None)]
                        for g in range(2):
                            groups.append((g, r * 128, 128, kts, m == 1))
                if m == 1:
                    pre_out = [ring_load(("eout", i, 0)), ring_load(("eout", i, 1))]
                attn_phase(i, m, groups)
                ada_step()
            while ada_q:
                ada_step()
            stage(sb_ + 9)
            out_proj("eout", i, l, cond, pre=pre_out)

        def odd_layer(i, l, cond, lat, mid=None, sb_=0):
            stage(sb_ + 4)
            for c in range(2):
                slot = ring_load(("oin", i, 2 * c))
                fm_proj(slot, bC, "bc", None)
                ada_step()
                slot = ring_load(("oin", i, 2 * c + 1))
                fm_proj(slot, G[c], ("g", c), AF.Silu)
                ada_step()
                for tt in range(8):
                    for gg in range(2):
                        k, bank = mm_bank()
                        for kk in range(2):
                            P.op("pe", lambda e, kk=kk, gg=gg, tt=tt, bank=bank: e.matmul(
                                bank, lhsT=bC[:, 2 * gg + kk, tts(tt)], rhs=C["csc"][:, kk, :], start=(kk == 0), stop=(kk == 1)),
                                reads=[("bc", 2 * gg + kk, tt), ("c", "csc")], writes=[("ps", k)])
                        if alt() % 2 == 0:
                            P.op("act", lambda e, tt=tt, gg=gg, bank=bank: e.copy(out=PQ[:, tt, gg, :], in_=bank),
                                 reads=[("ps", k)], writes=[("pq", tt, gg)])
                        else:
                            P.op("dve", lambda e, tt=tt, gg=gg, bank=bank: e.tensor_copy(out=PQ[:, tt, gg, :], in_=bank),
                                 reads=[("ps", k)], writes=[("pq", tt, gg)])
                if not lat:
                    for sp_ in range(2):
                        for gg in range(2):
                            for half in range(2):
                                k, bank = mm_bank()
                                for s2 in range(2):
                                    s = 2 * sp_ + s2
                                    for pt in range(2):
                                        P.op("pe", lambda e, s2=s2, s=s, pt=pt, gg=gg, half=half, bank=bank: e.matmul(
                                            bank[:, s2 * 256:(s2 + 1) * 256], lhsT=PQ[:, 2 * s + pt, gg, half * 128:(half + 1) * 128],
                                            rhs=C["csx"][:, pt, :], start=(pt == 0), stop=False),
                                            reads=[("pq", 2 * s + pt, gg), ("c", "csx")], writes=[("ps", k)])
                                        P.op("pe", lambda e, s2=s2, s=s, pt=pt, gg=gg, half=half, bank=bank: e.matmul(
                                            bank[:, s2 * 256:(s2 + 1) * 256], lhsT=PQ[:, 2 * s + pt, gg, 256 + half * 128:256 + (half + 1) * 128],
                                            rhs=C["nssx"][:, pt, :], start=False, stop=(pt == 1)),
                                            reads=[("pq", 2 * s + pt, gg), ("c", "nssx")], writes=[("ps", k)])
                                P.op("dve", lambda e, c=c, gg=gg, half=half, sp_=sp_, bank=bank: e.tensor_tensor(
                                    out=G[c][:, 2 * gg + half, tcs(sp_)], in0=bank, in1=G[c][:, 2 * gg + half, tcs(sp_)], op=ALU.mult),
                                    reads=[("ps", k), ("g", c, 2 * gg + half, sp_)], writes=[("g", c, 2 * gg + half, sp_)])
                else:
                    for tc in range(2):
                        slotC = ring_load(("dft", 2 * tc))
                        slotS = ring_load(("dft", 2 * tc + 1))
                        for gg in range(2):
                            for half in range(2):
                                k, bank = mm_bank()
                                for pt in range(8):
                                    P.op("pe", lambda e, pt=pt, gg=gg, half=half, bank=bank, slotC=slotC: e.matmul(
                                        bank, lhsT=PQ[:, pt, gg, half * 128:(half + 1) * 128], rhs=ring[slotC][:, pt, :],
                                        start=(pt == 0), stop=False),
                                        reads=[("pq", pt, gg), ("ring", slotC)], writes=[("ps", k)])
                                    P.op("pe", lambda e, pt=pt, gg=gg, half=half, bank=bank, slotS=slotS: e.matmul(
                                        bank, lhsT=PQ[:, pt, gg, 256 + half * 128:256 + (half + 1) * 128], rhs=ring[slotS][:, pt, :],
                                        start=False, stop=(pt == 7)),
                                        reads=[("pq", pt, gg), ("ring", slotS)], writes=[("ps", k)])
                                P.op("dve", lambda e, c=c, gg=gg, half=half, tc=tc, bank=bank: e.tensor_tensor(
                                    out=G[c][:, 2 * gg + half, tcs(tc)], in0=bank, in1=G[c][:, 2 * gg + half, tcs(tc)], op=ALU.mult),
                                    reads=[("ps", k), ("g", c, 2 * gg + half, tc)], writes=[("g", c, 2 * gg + half, tc)])
                ada_step()
            while ada_q:
                ada_step()
            out_proj("oout", i, l, cond)

        def final(y_d):
            rms_stats()
            for ft in range(8):
                for tc in range(2):
                    P.op("dve", lambda e, ft=ft, tc=tc: e.scalar_tensor_tensor(
                        out=xT[:, ft, tcs(tc)], in0=xT[:, ft, tcs(tc)], scalar=C["fg"][:, ft:ft + 1],
                        in1=rstd[:, tcs(tc)], op0=ALU.mult, op1=ALU.mult),
                        reads=[("xT", ft, tc), ("c", "fg"), ("rstd", tc)], writes=[("xT", ft, tc)])
            for tt in range(8):
                b = tt % 2
                for half in range(2):
                    k, bank = mm_bank()
                    for q in range(4):
                        ft = half * 4 + q
                        P.op("pe", lambda e, q=q, ft=ft, tt=tt, bank=bank: e.transpose(
                            out=bank[:, q * 128:(q + 1) * 128], in_=xT[:, ft, tts(tt)], identity=C["ident"][:]),
                            reads=[("xT", ft, tt // 4), ("c", "ident")], writes=[("ps", k)])
                    if half == 0:
                        P.op("act", lambda e, b=b, half=half, bank=bank: e.copy(out=xst[b][:, tcs(half)], in_=bank),
                             reads=[("ps", k)], writes=[("xst", b)])
                    else:
                        P.op("dve", lambda e, b=b, half=half, bank=bank: e.tensor_copy(out=xst[b][:, tcs(half)], in_=bank),
                             reads=[("ps", k)], writes=[("xst", b)])
                P.dma("sp", lambda e, tt=tt, b=b: e.dma_start(out=y_d[tt * 128:(tt + 1) * 128, :], in_=xst[b][:]),
                      ("yst", b), reads=[("xst", b)], is_output=True)

        try:
            stage(1)
            for j in range(6):
                ada_chunk(0, j)
            for lat in (False, True):
                cond = 1 if lat else 0
                stage(2 + 100 * lat)
                load_x(xl_d if lat else xc_d)
                for l in range(KLAYERS):
                    stage(3 + 10 * l + 100 * lat)
                    norm_mod(l, cond)
                    if not lat and l < 3:
                        ada_q.extend((l + 1, j) for j in range(6))
                    if l % 2 == 0:
                        even_layer(l // 2, l, cond, lat, sb_=10 * l + 100 * lat)
                    else:
                        odd_layer(l // 2, l, cond, lat, sb_=10 * l + 100 * lat)
                stage(50 + 100 * lat)
                final(yl_d if lat else yc_d)
        except _Stop:
            pass

        P.finalize()
        P.run_block(lambda name: es.enter_context(nc.semaphore(name)))
    return nc


def _fm(v):
    v = np.asarray(v, np.float32)
    lead = v.shape[:-1]
    k = v.shape[-1] // 128
    a = v.reshape(*lead, k, 128)
    a = np.moveaxis(a, -1, 0)
    return np.ascontiguousarray(a)


def _tables():
    t = {}
    t["ident"] = np.eye(128, dtype=np.float32)
    n = np.arange(256)
    ang = 2 * np.pi * ((n[:, None] * n[None, :]) % 256) / 256.0
    cc = np.cos(ang) / 16.0
    sc = np.sin(ang) / 16.0
    csc = np.concatenate([cc, sc], axis=1).astype(np.float32)
    t["csc"] = np.ascontiguousarray(csc.reshape(2, 128, 512).transpose(1, 0, 2))
    t["csx"] = np.ascontiguousarray(cc.astype(np.float32).reshape(2, 128, 256).transpose(1, 0, 2))
    t["nssx"] = np.ascontiguousarray((-sc).astype(np.float32).reshape(2, 128, 256).transpose(1, 0, 2))
    n = np.arange(1024)
    ang = 2 * np.pi * ((n[:, None] * n[None, :]) % 1024) / 1024.0
    t["C1024"] = (np.cos(ang) / 32.0).astype(np.float32)
    t["nS1024"] = (-np.sin(ang) / 32.0).astype(np.float32)
    rows = 1024 // 64
    row = np.repeat(np.arange(rows), 64).astype(np.float32)
    col = np.tile(np.arange(64), rows).astype(np.float32)
    inv = (1.0 / (np.float32(10000.0) ** (np.arange(0, 32, 2, dtype=np.float32) / np.float32(32)))).astype(np.float32)
    angr = (row[:, None] * inv).astype(np.float32)
    angc = (col[:, None] * inv).astype(np.float32)
    cr, sr, cl, sl = np.cos(angr), np.sin(angr), np.cos(angc), np.sin(angc)
    Cf = np.concatenate([cr, cr, cl, cl], axis=1).astype(np.float32)
    Sn = np.concatenate([-sr, -sl], axis=1).astype(np.float32)
    Sp = np.concatenate([sr, sl], axis=1).astype(np.float32)
    t["ropeC"] = np.ascontiguousarray(Cf.reshape(8, 128, 64).transpose(1, 0, 2))
    t["ropeSn"] = np.ascontiguousarray(Sn.reshape(8, 128, 32).transpose(1, 0, 2))
    t["ropeSp"] = np.ascontiguousarray(Sp.reshape(8, 128, 32).transpose(1, 0, 2))
    ii = np.arange(128)
    t["mlo"] = (ii[:, None] >= ii[None, :]).astype(np.float32)
    t["mhi"] = (ii[:, None] <= ii[None, :]).astype(np.float32)
    return t


_NC_CACHE = {}


def kernel(x_prompt, x_sample, cache_k_a, cache_v_a, cache_k_b, cache_v_b, c, c_ctx,
           norm_g, ada_w, ada_b, even_w_in, even_w_out, qk_g_q, qk_g_k, sink_logit,
           odd_w_in, odd_w_out, final_g):
    f = lambda a: np.asarray(a, np.float32)
    x_prompt, x_sample = f(x_prompt), f(x_sample)
    tb = _tables()
    wst = np.empty((NCH, 1024, 512), np.float32)
    ada_w = f(ada_w)
    for l in range(4):
        for j in range(6):
            wst[CH_IDS[("ada", l, j)]] = ada_w[l][:, j * 512:(j + 1) * 512]
    ewi, ewo, owi, owo = f(even_w_in), f(even_w_out), f(odd_w_in), f(odd_w_out)
    for i in range(2):
        w = ewi[i]
        qa, ka, va, ga = w[:, 0:512], w[:, 512:640], w[:, 640:768], w[:, 768:1280]
        qb, kb, vb, gb = w[:, 1280:1792], w[:, 1792:1920], w[:, 1920:2048], w[:, 2048:2560]
        chunks = [np.concatenate([ka, va, kb, vb], axis=1), qa, ga, qb, gb]
        for j in range(5):
            wst[CH_IDS[("ein", i, j)]] = chunks[j]
        for j in range(2):
            wst[CH_IDS[("eout", i, j)]] = ewo[i][:, j * 512:(j + 1) * 512]
        w = owi[i]
        chunks = [w[:, 0:512], w[:, 1024:1536], w[:, 512:1024], w[:, 1536:2048]]
        for j in range(4):
            wst[CH_IDS[("oin", i, j)]] = chunks[j]
        for j in range(2):
            wst[CH_IDS[("oout", i, j)]] = owo[i][:, j * 512:(j + 1) * 512]
    for tc in range(2):
        wst[CH_IDS[("dft", 2 * tc)]] = tb["C1024"][:, tc * 512:(tc + 1) * 512]
        wst[CH_IDS[("dft", 2 * tc + 1)]] = tb["nS1024"][:, tc * 512:(tc + 1) * 512]

    shared = {
        "ident": tb["ident"], "csc": tb["csc"], "csx": tb["csx"], "nssx": tb["nssx"],
        "ropeC": tb["ropeC"], "ropeSn": tb["ropeSn"], "ropeSp": tb["ropeSp"], "mlo": tb["mlo"], "mhi": tb["mhi"],
        "ng": _fm(norm_g), "fg": _fm(final_g), "adab": _fm(ada_b),
        "gq": np.ascontiguousarray(np.broadcast_to(f(qk_g_q)[None], (128, 2, 64))),
        "gk": np.ascontiguousarray(np.broadcast_to(f(qk_g_k)[None], (128, 2, 64))),
        "snk": np.ascontiguousarray(np.broadcast_to(f(sink_logit)[None], (128, 2, 8))),
        "wst": wst,
    }
    caches = [f(cache_k_a), f(cache_v_a), f(cache_k_b), f(cache_v_b)]
    in_maps = []
    for core in range(8):
        b = core % 2
        d = dict(shared)
        d["xc"] = np.ascontiguousarray(x_prompt[4 * core:4 * core + 4].reshape(1024, 1024))
        d["xl"] = np.ascontiguousarray(x_sample[b])
        d["cache"] = np.ascontiguousarray(np.stack([np.stack([cc_[b, i].reshape(256, 128) for cc_ in caches]) for i in range(2)]))
        cv = np.stack([f(c_ctx), f(c)[b]])
        d["cvT"] = np.ascontiguousarray(cv.reshape(2, 8, 128).transpose(2, 1, 0))
        in_maps.append(d)

    if "nc" not in _NC_CACHE:
        _NC_CACHE["nc"] = build_nc()
    nc = _NC_CACHE["nc"]
    res = run_bass_kernel_spmd(nc, in_maps, core_ids=list(range(8)))
    outs = res.results

    y_prompt = np.empty((32, 256, 1024), np.float32)
    y_sample = np.empty((2, 1024, 1024), np.float32)
    nk = [np.empty((32, 2, 256, 2, 64), np.float32) for _ in range(4)]
    for core in range(8):
        r = outs[core]
        y_prompt[4 * core:4 * core + 4] = np.asarray(r["yc"]).reshape(4, 256, 1024)
        if core < 2:
            y_sample[core] = np.asarray(r["yl"])
        kvo = np.asarray(r["kvo"]).reshape(2, 4, 256, 4, 2, 64)
        for kind in range(4):
            nk[kind][4 * core:4 * core + 4] = kvo[:, :, :, kind].transpose(1, 0, 2, 3, 4)
    return (y_prompt, y_sample, nk[0], nk[1], nk[2], nk[3])
```
